# Optimizing a Trainium2 kernel written in Bass

```python
import math
import jax, jax.numpy as jnp
from jax import lax
import numpy as np

D_MODEL = 1024
BATCH = 2
SEQ = 8192
DEPTH = 2

N_MIXERS = 2
N_ATTN_LAYERS = (DEPTH + 1) // 2
N_SSD_LAYERS = DEPTH // 2

MEM_LEN = 256
EPS = 1e-6

DA_HEAD_DIM = 64
DA_N_HEADS = D_MODEL // (2 * DA_HEAD_DIM)
DA_QK_WIDTH = DA_N_HEADS * 2 * DA_HEAD_DIM
DA_V_WIDTH = DA_N_HEADS * 2 * DA_HEAD_DIM
DA_IN_WIDTH = 2 * DA_QK_WIDTH + DA_V_WIDTH
Q_BLOCK = 128
ROPE_THETA = 10000.0

SSD_EXPAND = 2
SSD_D_INNER = SSD_EXPAND * D_MODEL
SSD_HEAD_DIM = 64
SSD_N_HEADS = SSD_D_INNER // SSD_HEAD_DIM
SSD_N_GROUPS = 8
SSD_HEADS_PER_GROUP = SSD_N_HEADS // SSD_N_GROUPS
SSD_D_STATE = 128
SSD_CONV_WIDTH = 4
SSD_CONV_DIM = SSD_D_INNER + 2 * SSD_N_GROUPS * SSD_D_STATE
SSD_IN_WIDTH = SSD_D_INNER + SSD_CONV_DIM + SSD_N_HEADS
SSD_CHUNK = 128

XA_N_HEADS = 4
XA_HEAD_DIM = D_MODEL // XA_N_HEADS

FFN_HIDDEN = 2816

kernel_name = "hybrid_diffattn_mamba2_macaron_memxattn"


def rmsnorm(x, w):
    xf = x.astype(jnp.float32)
    y = xf * lax.rsqrt(jnp.mean(xf * xf, axis=-1, keepdims=True) + EPS)
    return (y * w.astype(jnp.float32)).astype(x.dtype)


def swiglu(h, w_gate, w_up, w_down):
    return (jax.nn.silu(h @ w_gate) * (h @ w_up)) @ w_down


def rope_tables(positions):
    inv = 1.0 / (ROPE_THETA ** (jnp.arange(0, DA_HEAD_DIM, 2, dtype=jnp.float32) / DA_HEAD_DIM))
    ang = positions.astype(jnp.float32)[..., None] * inv
    return jnp.cos(ang), jnp.sin(ang)


def apply_rope(x, cos, sin):
    xf = x.astype(jnp.float32)
    half = xf.shape[-1] // 2
    x1, x2 = xf[..., :half], xf[..., half:]
    c, s = cos[:, :, None, :], sin[:, :, None, :]
    return jnp.concatenate([x1 * c - x2 * s, x2 * c + x1 * s], axis=-1).astype(x.dtype)


def lambda_init_fn(layer_idx):
    return 0.8 - 0.6 * math.exp(-0.3 * layer_idx)


def diff_attention(h, cos, sin, w_in, lq1, lk1, lq2, lk2, subln, w_out, lambda_init):
    B, S, _ = h.shape
    H, d = DA_N_HEADS, DA_HEAD_DIM
    proj = h @ w_in
    q, k, v = jnp.split(proj, [DA_QK_WIDTH, 2 * DA_QK_WIDTH], axis=-1)
    q = apply_rope(q.reshape(B, S, 2 * H, d), cos, sin) * (d ** -0.5)
    k = apply_rope(k.reshape(B, S, 2 * H, d), cos, sin)
    q = q.reshape(B, S, H, 2, d)
    k = k.reshape(B, S, H, 2, d)
    v = v.reshape(B, S, H, 2 * d)
    lam = (jnp.exp(jnp.sum(lq1.astype(jnp.float32) * lk1.astype(jnp.float32)))
           - jnp.exp(jnp.sum(lq2.astype(jnp.float32) * lk2.astype(jnp.float32)))
           + lambda_init)
    n_blk = S // Q_BLOCK
    q_blocks = q.reshape(B, n_blk, Q_BLOCK, H, 2, d).transpose(1, 0, 2, 3, 4, 5)
    k_pos = jnp.arange(S)

    def block(args):
        qb, bi = args
        q_pos = bi * Q_BLOCK + jnp.arange(Q_BLOCK)
        s = jnp.einsum("bqhcd,bkhcd->bhcqk", qb, k).astype(jnp.float32)
        mask = k_pos[None, :] <= q_pos[:, None]
        s = jnp.where(mask, s, -jnp.inf)
        p = jax.nn.softmax(s, axis=-1)
        a = p[:, :, 0] - lam * p[:, :, 1]
        return jnp.einsum("bhqk,bkhe->bqhe", a.astype(v.dtype), v)

    o = lax.map(block, (q_blocks, jnp.arange(n_blk)))
    o = o.transpose(1, 0, 2, 3, 4).reshape(B, S, H, 2 * d)
    o = rmsnorm(o, subln) * (1.0 - lambda_init)
    return o.reshape(B, S, H * 2 * d) @ w_out


def causal_depthwise_conv(u, w, b):
    C = u.shape[-1]
    out = lax.conv_general_dilated(
        u, w[:, None, :], window_strides=(1,), padding=[(SSD_CONV_WIDTH - 1, 0)],
        dimension_numbers=("NWC", "WIO", "NWC"), feature_group_count=C)
    return out + b


def ssd_chunked_scan(x, a, Bm, Cm):
    Bsz, S, G, R, P = x.shape
    N = Bm.shape[-1]
    L = SSD_CHUNK
    c = S // L
    x = x.reshape(Bsz, c, L, G, R, P)
    Bm = Bm.reshape(Bsz, c, L, G, N)
    Cm = Cm.reshape(Bsz, c, L, G, N)
    a_cs = jnp.cumsum(a.reshape(Bsz, c, L, G, R), axis=2).transpose(0, 3, 4, 1, 2)
    seg = a_cs[..., :, None] - a_cs[..., None, :]
    tril = jnp.tril(jnp.ones((L, L), dtype=bool))
    decay = jnp.exp(jnp.where(tril, seg, -jnp.inf))
    cb = jnp.einsum("bclgn,bcsgn->bgcls", Cm, Bm)
    y_diag = jnp.einsum("bgcls,bgrcls,bcsgrp->bclgrp", cb, decay, x)
    decay_to_end = jnp.exp(a_cs[..., -1:] - a_cs)
    states = jnp.einsum("bclgn,bgrcl,bclgrp->cbgrpn", Bm, decay_to_end, x)
    chunk_decay = jnp.exp(a_cs[..., -1]).transpose(3, 0, 1, 2)

    def step(carry, inp):
        st, dec = inp
        return carry * dec[..., None, None] + st, carry

    init = jnp.zeros(states.shape[1:], dtype=states.dtype)
    _, prev_states = lax.scan(step, init, (states, chunk_decay))
    y_off = jnp.einsum("bclgn,cbgrpn,bgrcl->bclgrp", Cm, prev_states, jnp.exp(a_cs))
    return (y_diag + y_off).reshape(Bsz, S, G, R, P)


def ssd_mixer(h, w_in, conv_w, conv_b, dt_bias, A_log, D_skip, gnorm, w_out):
    B, S, _ = h.shape
    G, R, N, P = SSD_N_GROUPS, SSD_HEADS_PER_GROUP, SSD_D_STATE, SSD_HEAD_DIM
    proj = h @ w_in
    z, xbc, dt = jnp.split(proj, [SSD_D_INNER, SSD_D_INNER + SSD_CONV_DIM], axis=-1)
    xbc = jax.nn.silu(causal_depthwise_conv(xbc, conv_w, conv_b))
    xs, Bm, Cm = jnp.split(xbc, [SSD_D_INNER, SSD_D_INNER + G * N], axis=-1)
    xs = xs.reshape(B, S, G, R, P)
    Bm = Bm.reshape(B, S, G, N)
    Cm = Cm.reshape(B, S, G, N)
    dt = jax.nn.softplus((dt + dt_bias).astype(jnp.float32)).reshape(B, S, G, R)
    A = -jnp.exp(A_log.astype(jnp.float32)).reshape(G, R)
    y = ssd_chunked_scan(xs * dt[..., None], dt * A, Bm, Cm)
    y = y + D_skip.reshape(G, R)[:, :, None].astype(jnp.float32) * xs
    y = y.reshape(B, S, SSD_D_INNER) * jax.nn.silu(z.astype(jnp.float32))
    yg = y.reshape(B, S, G, SSD_D_INNER // G)
    yg = yg * lax.rsqrt(jnp.mean(yg * yg, axis=-1, keepdims=True) + EPS)
    y = (yg.reshape(B, S, SSD_D_INNER) * gnorm.astype(jnp.float32)).astype(h.dtype)
    return y @ w_out


def memory_cross_attention(h, mem, mem_norm, w_q, w_kv, w_o):
    B, S, _ = h.shape
    M = mem.shape[1]
    m = rmsnorm(mem, mem_norm)
    q = (h @ w_q).reshape(B, S, XA_N_HEADS, XA_HEAD_DIM)
    k, v = jnp.split(m @ w_kv, 2, axis=-1)
    k = k.reshape(B, M, XA_N_HEADS, XA_HEAD_DIM)
    v = v.reshape(B, M, XA_N_HEADS, XA_HEAD_DIM)
    s = jnp.einsum("bqhd,bmhd->bhqm", q, k).astype(jnp.float32) * (XA_HEAD_DIM ** -0.5)
    p = jax.nn.softmax(s, axis=-1)
    o = jnp.einsum("bhqm,bmhd->bqhd", p.astype(v.dtype), v).reshape(B, S, D_MODEL)
    return o @ w_o


def setup_inputs(seed: int = 0) -> dict:
    key = jax.random.key(seed)
    ks = iter(jax.random.split(key, 64))

    def w(shape, fan_in):
        return jax.random.normal(next(ks), shape, jnp.float32) * (fan_in ** -0.5)

    def gain(shape):
        return 1.0 + 0.02 * jax.random.normal(next(ks), shape, jnp.float32)

    def small(shape, scale):
        return scale * jax.random.normal(next(ks), shape, jnp.float32)

    x = jax.random.normal(next(ks), (BATCH, SEQ, D_MODEL), jnp.float32)
    mem = jax.random.normal(next(ks), (BATCH, MEM_LEN, D_MODEL), jnp.float32)
    positions = jnp.broadcast_to(jnp.arange(SEQ, dtype=jnp.int32), (BATCH, SEQ))

    dt0 = jnp.exp(jax.random.uniform(next(ks), (N_SSD_LAYERS, SSD_N_HEADS), jnp.float32)
                  * (math.log(0.1) - math.log(0.001)) + math.log(0.001))
    ssd_dt_bias = dt0 + jnp.log(-jnp.expm1(-dt0))
    ssd_A_log = jnp.log(jax.random.uniform(next(ks), (N_SSD_LAYERS, SSD_N_HEADS), jnp.float32,
                                           minval=1.0, maxval=16.0))

    return {
        "x": x,
        "mem": mem,
        "positions": positions,
        "ffn1_norm": gain((DEPTH, D_MODEL)),
        "ffn1_w_gate": w((DEPTH, D_MODEL, FFN_HIDDEN), D_MODEL),
        "ffn1_w_up": w((DEPTH, D_MODEL, FFN_HIDDEN), D_MODEL),
        "ffn1_w_down": w((DEPTH, FFN_HIDDEN, D_MODEL), FFN_HIDDEN),
        "mix_norm": gain((DEPTH, D_MODEL)),
        "da_w_in": w((N_ATTN_LAYERS, D_MODEL, DA_IN_WIDTH), D_MODEL),
        "da_lambda_q1": small((N_ATTN_LAYERS, DA_HEAD_DIM), 0.1),
        "da_lambda_k1": small((N_ATTN_LAYERS, DA_HEAD_DIM), 0.1),
        "da_lambda_q2": small((N_ATTN_LAYERS, DA_HEAD_DIM), 0.1),
        "da_lambda_k2": small((N_ATTN_LAYERS, DA_HEAD_DIM), 0.1),
        "da_subln": gain((N_ATTN_LAYERS, 2 * DA_HEAD_DIM)),
        "da_w_out": w((N_ATTN_LAYERS, DA_V_WIDTH, D_MODEL), DA_V_WIDTH),
        "ssd_w_in": w((N_SSD_LAYERS, D_MODEL, SSD_IN_WIDTH), D_MODEL),
        "ssd_conv_w": w((N_SSD_LAYERS, SSD_CONV_WIDTH, SSD_CONV_DIM), SSD_CONV_WIDTH),
        "ssd_conv_b": small((N_SSD_LAYERS, SSD_CONV_DIM), 0.01),
        "ssd_dt_bias": ssd_dt_bias,
        "ssd_A_log": ssd_A_log,
        "ssd_D": gain((N_SSD_LAYERS, SSD_N_HEADS)),
        "ssd_gnorm": gain((N_SSD_LAYERS, SSD_D_INNER)),
        "ssd_w_out": w((N_SSD_LAYERS, SSD_D_INNER, D_MODEL), SSD_D_INNER),
        "xa_norm": gain((DEPTH, D_MODEL)),
        "xa_mem_norm": gain((DEPTH, D_MODEL)),
        "xa_w_q": w((DEPTH, D_MODEL, D_MODEL), D_MODEL),
        "xa_w_kv": w((DEPTH, D_MODEL, 2 * D_MODEL), D_MODEL),
        "xa_w_o": w((DEPTH, D_MODEL, D_MODEL), D_MODEL),
        "ffn2_norm": gain((DEPTH, D_MODEL)),
        "ffn2_w_gate": w((DEPTH, D_MODEL, FFN_HIDDEN), D_MODEL),
        "ffn2_w_up": w((DEPTH, D_MODEL, FFN_HIDDEN), D_MODEL),
        "ffn2_w_down": w((DEPTH, FFN_HIDDEN, D_MODEL), FFN_HIDDEN),
        "final_norm": gain((D_MODEL,)),
    }


def reference(x, mem, positions, ffn1_norm, ffn1_w_gate, ffn1_w_up, ffn1_w_down, mix_norm,
              da_w_in, da_lambda_q1, da_lambda_k1, da_lambda_q2, da_lambda_k2, da_subln, da_w_out,
              ssd_w_in, ssd_conv_w, ssd_conv_b, ssd_dt_bias, ssd_A_log, ssd_D, ssd_gnorm, ssd_w_out,
              xa_norm, xa_mem_norm, xa_w_q, xa_w_kv, xa_w_o,
              ffn2_norm, ffn2_w_gate, ffn2_w_up, ffn2_w_down, final_norm):
    cos, sin = rope_tables(positions)
    h = x
    for i in range(DEPTH):
        h = h + 0.5 * swiglu(rmsnorm(h, ffn1_norm[i]), ffn1_w_gate[i], ffn1_w_up[i], ffn1_w_down[i])
        hn = rmsnorm(h, mix_norm[i])
        j = i // N_MIXERS
        if i % N_MIXERS == 0:
            h = h + diff_attention(hn, cos, sin, da_w_in[j], da_lambda_q1[j], da_lambda_k1[j],
                                   da_lambda_q2[j], da_lambda_k2[j], da_subln[j], da_w_out[j],
                                   lambda_init_fn(i))
        else:
            h = h + ssd_mixer(hn, ssd_w_in[j], ssd_conv_w[j], ssd_conv_b[j], ssd_dt_bias[j],
                              ssd_A_log[j], ssd_D[j], ssd_gnorm[j], ssd_w_out[j])
        h = h + memory_cross_attention(rmsnorm(h, xa_norm[i]), mem, xa_mem_norm[i],
                                       xa_w_q[i], xa_w_kv[i], xa_w_o[i])
        h = h + 0.5 * swiglu(rmsnorm(h, ffn2_norm[i]), ffn2_w_gate[i], ffn2_w_up[i], ffn2_w_down[i])
    return rmsnorm(h, final_norm)
```

```python
import numpy as np
from contextlib import ExitStack
import concourse.bass as bass
import concourse.mybir as mybir
from concourse.bass_utils import run_bass_kernel_spmd

F32 = mybir.dt.float32
BF16 = mybir.dt.bfloat16
AF = mybir.ActivationFunctionType
ALU = mybir.AluOpType

NCORES = 8
D = 1024
TOK = 2048
NT = TOK // 128
FH = 2816
EPS = 1e-6
EPOCH = 30000
ENGS = ('pe', 'act', 'dve', 'pool', 'sp')


class Buf:
    __slots__ = ('h', 'name', 'w', 'r', 'dsem', 'dcnt', 'is_dram', 'is_psum')

    def __init__(self, h, name, is_dram=False):
        self.h = h
        self.name = name
        self.is_dram = is_dram
        self.is_psum = False
        self.w = {}
        self.r = {}
        self.dsem = {}
        self.dcnt = {}

    def __getitem__(self, idx):
        return self.h[idx]


class Prog:
    def __init__(self, nc, es):
        self.nc = nc
        self.es = es
        self.q = {e: [] for e in ENGS}
        self.cnt = {e: 0 for e in ENGS}
        self.pending = {e: False for e in ENGS}
        self.waited = {e: {} for e in ENGS}
        self.sems = {}
        self.bufs = []
        self.colls = {}
        self.free_dsems = {}
        self.scope_stack = []
        self.root_es = es

    def sb(self, name, shape, dt):
        self.nalloc = getattr(self, 'nalloc', 0) + 1
        name = "%s_%d" % (name, self.nalloc)
        b = self._sb(name, shape, dt)
        self.bufs.append(b)
        if self.scope_stack:
            self.scope_stack[-1].append(b)
        return b

    def _sb(self, name, shape, dt):
        return Buf(self.es.enter_context(self.nc.sbuf_tensor("sb_" + name, list(shape), dt)), "sb_" + name)

    def ps(self, name, shape, dt=F32):
        b = Buf(self.es.enter_context(self.nc.psum_tensor("ps_" + name, list(shape), dt)), "ps_" + name)
        b.is_psum = True
        return b

    def dram(self, name, shape, dt, kind):
        b = Buf(self.nc.dram_tensor(name, list(shape), dt, kind=kind), name, True)
        self.bufs.append(b)
        return b

    def sem(self, key):
        if key not in self.sems:
            self.sems[key] = self.root_es.enter_context(self.nc.semaphore("s%d" % len(self.sems)))
        return self.sems[key]

    def _need(self, e, key, val):
        if key == ('e', 'pe') and e == 'pe':
            return
        if self.waited[e].get(key, 0) >= val:
            return
        self.waited[e][key] = val
        self.q[e].append(('w', key, val))

    def _deps(self, e, reads, writes):
        for b in reads:
            for k, v in b.w.items():
                self._need(e, k, v)
            if b.is_psum:
                for k, v in b.r.items():
                    if k != ('e', e):
                        self._need(e, k, v)
        for b in writes:
            for k, v in b.w.items():
                self._need(e, k, v)
            for k, v in b.r.items():
                self._need(e, k, v)

    def op(self, e, fn, reads=(), writes=(), inc=True):
        self._deps(e, reads, writes)
        key = ('e', e)
        val = self.cnt[e] + 1
        if inc:
            self.cnt[e] = val
            self.pending[e] = False
        else:
            self.pending[e] = True
        self.q[e].append(('i', fn, key if inc else None, 1))
        for b in writes:
            b.w = {key: val}
            b.r = {}
        for b in reads:
            if b.r.get(key, 0) < val:
                b.r[key] = val

    def dma(self, e, ob, out, ib, in_, accumulate_w=False, **kw):
        self._deps(e, [ib], [ob])
        owner = ib if ob.is_dram else ob
        if e not in owner.dsem:
            fl = self.free_dsems.setdefault(e, [])
            if fl:
                owner.dsem[e], owner.dcnt[e] = fl.pop()
            else:
                owner.dsem[e], owner.dcnt[e] = ('d', owner.name + "@" + e), 0
        owner.dcnt[e] += 16
        key, val = owner.dsem[e], owner.dcnt[e]
        self.q[e].append(('i', lambda eng: eng.dma_start(out=out, in_=in_, **kw), key, 16))
        if accumulate_w:
            ob.w[key] = val
        else:
            ob.w = {key: val}
            ob.r = {}
        ib.r[key] = val

    def wait_all(self, e, bufs):
        self._deps(e, bufs, bufs)

    def barrier(self):
        for e in ENGS:
            for f in ENGS:
                if self.cnt[f] > 0:
                    self._need(e, ('e', f), self.cnt[f])
            for b in self.bufs:
                for e2, k2 in b.dsem.items():
                    self._need(e, k2, b.dcnt[e2])
            for k, v in self.colls.items():
                self._need(e, k, v)

    def scope(self):
        prog = self

        class _Scope:
            def __enter__(self_):
                self_.old = prog.es
                prog.es = ExitStack()
                prog.scope_stack.append([])
                return self_

            def __exit__(self_, *a):
                prog.barrier()
                for b in prog.scope_stack.pop():
                    for e2, k2 in b.dsem.items():
                        prog.free_dsems.setdefault(e2, []).append((k2, b.dcnt[e2]))
                    b.dsem = {}
                    b.dcnt = {}
                prog.es.close()
                prog.es = self_.old
                return False
        return _Scope()

    def coll(self, kind, ib, in_ap, ob, out_ap, groups):
        self._deps('pool', [ib], [ob])
        key = ('c', len(self.colls))
        self.colls[key] = 1
        self.q['pool'].append(('i', lambda eng: eng.collective_compute(kind, ALU.bypass, replica_groups=groups,
                                                                       ins=[in_ap], outs=[out_ap]), key, 1))
        ob.w = {key: 1}
        ob.r = {}
        ib.r[key] = 1

    def _semh(self, key, val):
        if key[0] == 'c':
            return self.sem((key, 0)), val
        if key[0] == 'e':
            ep = (val - 1) // EPOCH
            return self.sem((key, ep)), val - ep * EPOCH
        ep = (val - 1) // (EPOCH * 16)
        return self.sem((key, ep)), val - ep * EPOCH * 16

    def emit(self, e, eng):
        for item in self.q[e]:
            if item[0] == 'w':
                s, v = self._semh(item[1], item[2])
                eng.wait_ge(s, v)
            else:
                _, fn, key, step = item
                ins = fn(eng)
                if key is not None:
                    if key[0] == 'c':
                        ins.then_inc(self.sem((key, 0)))
                        continue
                    if key[0] == 'e':
                        self._emit_cnt[e] += 1
                        s, _ = self._semh(key, self._emit_cnt[e])
                    else:
                        self._emit_dcnt[key] = self._emit_dcnt.get(key, 0) + 16
                        s, _ = self._semh(key, self._emit_dcnt[key])
                    ins.then_inc(s, step)

    def finish(self):
        nc = self.nc
        for e in ENGS:
            assert not self.pending[e], "engine %s has trailing un-inc'd instructions" % e
        self._emit_cnt = {e: 0 for e in ENGS}
        self._emit_dcnt = {}
        for e in ENGS:
            for item in self.q[e]:
                if item[0] == 'w':
                    self._semh(item[1], item[2])
        with nc.Block() as block:
            @block.sync
            def _(eng):
                self.emit('sp', eng)

            @block.scalar
            def _(eng):
                self.emit('act', eng)

            @block.vector
            def _(eng):
                self.emit('dve', eng)

            @block.gpsimd
            def _(eng):
                self.emit('pool', eng)

            @block.tensor
            def _(eng):
                self.emit('pe', eng)

    def mm(self, ob, out, lb, lhsT, rb, rhs, start=True, stop=True, inc=None, **kw):
        self.op('pe', lambda e: e.matmul(out, lhsT, rhs, start=start, stop=stop, **kw),
                reads=[lb, rb], writes=[ob], inc=stop if inc is None else inc)

    def tr(self, ob, out, ib, in_, identb, ident, inc=True):
        self.op('pe', lambda e: e.transpose(out, in_, ident), reads=[ib, identb], writes=[ob], inc=inc)

    def act(self, ob, out, ib, in_, func, extra_reads=(), extra_writes=(), **kw):
        self.op('act', lambda e: e.activation(out, in_, func, **kw),
                reads=[ib] + list(extra_reads), writes=[ob] + list(extra_writes))

    def ts(self, e, ob, out, ib, in0, s1, s2, op0, op1=None, extra_reads=()):
        if op1 is None:
            self.op(e, lambda g: g.tensor_scalar(out, in0, s1, s2, op0), reads=[ib] + list(extra_reads), writes=[ob])
        else:
            self.op(e, lambda g: g.tensor_scalar(out, in0, s1, s2, op0, op1), reads=[ib] + list(extra_reads), writes=[ob])

    def tt(self, e, ob, out, ab, in0, bb, in1, op):
        self.op(e, lambda g: g.tensor_tensor(out, in0, in1, op), reads=[ab, bb], writes=[ob])

    def stt(self, e, ob, out, ab, in0, scalar, bb, in1, op0, op1, extra_reads=()):
        self.op(e, lambda g: g.scalar_tensor_tensor(out, in0, scalar, in1, op0, op1),
                reads=[ab, bb] + list(extra_reads), writes=[ob])

    def copy(self, e, ob, out, ib, in_):
        self.op(e, lambda g: g.tensor_copy(out, in_), reads=[ib], writes=[ob])


class St:
    pass


I32 = mybir.dt.int32
NEG = -30000.0
GROUPS = [[0, 1, 2, 3], [4, 5, 6, 7]]
ATTN_STOP = None
ATTN_HEADS = 8
ATTN_SRCS = 5
SSD_STOP = 99
ATTN_NOBIAS = True
WNAMES = (("ffn1_norm", [2, D]), ("ffn1_w_gate", [2, D, FH]), ("ffn1_w_up", [2, D, FH]), ("ffn1_w_down", [2, FH, D]),
          ("mix_norm", [2, D]), ("da_w_in", [1, D, 3072]), ("da_lambda_q1", [1, 64]), ("da_lambda_k1", [1, 64]),
          ("da_lambda_q2", [1, 64]), ("da_lambda_k2", [1, 64]), ("da_subln", [1, 128]), ("da_w_out", [1, D, D]),
          ("ssd_w_in", [1, D, 6176]), ("ssd_conv_w", [1, 4, 4096]), ("ssd_conv_b", [1, 4096]), ("ssd_dt_bias", [1, 32]),
          ("ssd_A_log", [1, 32]), ("ssd_D", [1, 32]), ("ssd_gnorm", [1, 2048]), ("ssd_w_out", [1, 2048, D]),
          ("xa_norm", [2, D]), ("xa_mem_norm", [2, D]), ("xa_w_q", [2, D, D]), ("xa_w_kv", [2, D, 2 * D]),
          ("xa_w_o", [2, D, D]),
          ("ffn2_norm", [2, D]), ("ffn2_w_gate", [2, D, FH]), ("ffn2_w_up", [2, D, FH]), ("ffn2_w_down", [2, FH, D]),
          ("final_norm", [D]))


def colvec(ap1d):
    return ap1d.rearrange("(k p o) -> p k o", p=128, o=1)


def build(stages=("ffn1_0", "attn", "xa_0", "ffn2_0", "ffn1_1", "ssd", "xa_1", "ffn2_1")):
    nc = bass.Bass("TRN2", target_bir_lowering=False)
    es = ExitStack()
    es.enter_context(nc.allow_low_precision("bf16 matmul operands, fp32 accumulation"))
    es.enter_context(nc.allow_non_contiguous_dma("small parameter vectors"))
    P = Prog(nc, es)
    S = St()

    def din(name, shape, dt=F32):
        return P.dram(name, shape, dt, "ExternalInput")

    x = din("x", [TOK, D])
    mem = din("mem", [256, D])
    pos = din("pos", [1, TOK], I32)
    cfg = din("cfg", [128, 16])
    utri_d = din("utri", [128, 128])
    halin = P.dram("halin", [128, 32], BF16, "Internal")
    halall = P.dram("halall", [512, 32], BF16, "Internal")
    ssdin = P.dram("ssdin", [128, 2048], F32, "Internal")
    ssdall = P.dram("ssdall", [512, 2048], F32, "Internal")
    ssdd = P.dram("ssdd", [128, 32], F32, "Internal")
    ssddall = P.dram("ssddall", [512, 32], F32, "Internal")
    ropec = din("ropec", [128, 4])
    ident_d = din("ident", [128, 128])
    negm_d = din("negmask", [128, 128])
    class _LazyW(dict):
        def __missing__(self, nm):
            self[nm] = din(nm, dict(WNAMES)[nm])
            return self[nm]
    W = _LazyW()
    out = P.dram("out", [TOK, D], F32, "ExternalOutput")
    VC = 16 * 129
    kin = [P.dram("kin%d" % i, [128, 2048], BF16, "Internal") for i in range(8)]
    vin = [P.dram("vin%d" % i, [128, VC], BF16, "Internal") for i in range(8)]
    kall = [P.dram("kall%d" % i, [512, 2048], BF16, "Internal") for i in range(8)]
    vall = [P.dram("vall%d" % i, [512, VC], BF16, "Internal") for i in range(8)]

    S.h = [P.sb("h%d" % t, [128, D], F32) for t in range(NT)]
    S.ident = P.sb("ident", [128, 128], BF16)
    S.ones = P.sb("ones", [128, 128], BF16)
    S.negm = P.sb("negm", [128, 128], BF16)
    S.cfg = P.sb("cfg", [128, 16], F32)
    S.junk = P.sb("junk", [128, D], BF16)
    S.ss = [P.sb("ss%d" % i, [128, 32], F32) for i in range(2)]
    S.gcol = {}
    for nm in ("ffn1_norm", "ffn2_norm", "mix_norm", "xa_norm", "xa_mem_norm"):
        for li in range(2):
            S.gcol[(nm, li)] = P.sb("gc_%s%d" % (nm, li), [128, 8, 1], F32)
    S.gfin = P.sb("gfin", [128, D], F32)
    S.bank = [P.ps("bank%d" % i, [128, 512]) for i in range(8)]
    S.rr = {}

    def rot(name, lst):
        i = S.rr.get(name, 0)
        S.rr[name] = i + 1
        return lst[i % len(lst)]

    def bf(bank):
        return bank[:].bitcast(BF16)

    P.dma('pool', S.ident, S.ident[:], ident_d, ident_d[:, :])
    P.dma('pool', S.negm, S.negm[:], negm_d, negm_d[:, :])
    P.dma('sp', S.cfg, S.cfg[:], cfg, cfg[:, :])
    P.op('dve', lambda g: g.memset(S.ones[:], 1.0), writes=[S.ones])
    for (nm, li), b in S.gcol.items():
        P.dma('sp', b, b[:], W[nm], colvec(W[nm][li, :]))
    P.dma('sp', S.gfin, S.gfin[:], W["final_norm"], W["final_norm"][:].partition_broadcast(128))
    for t in range(NT):
        P.dma('sp', S.h[t], S.h[t][:], x, x[t * 128:(t + 1) * 128, :])

    def rstd_cols(ssb, n, dim, lnexp=False):
        P.ts('dve', ssb, ssb[:, 16:16 + n], ssb, ssb[:, 0:n], 1.0 / dim, EPS, ALU.mult, ALU.add)
        if lnexp:
            P.act(ssb, ssb[:, 0:n], ssb, ssb[:, 16:16 + n], AF.Ln)
            P.act(ssb, ssb[:, 16:16 + n], ssb, ssb[:, 0:n], AF.Exp, scale=-0.5)
            return
        P.act(ssb, ssb[:, 0:n], ssb, ssb[:, 16:16 + n], AF.Sqrt)
        P.op('dve', lambda g: g.reciprocal(ssb[:, 16:16 + n], ssb[:, 0:n]), reads=[ssb], writes=[ssb])

    def rstd_group(tiles):
        ssb = rot("ss", S.ss)
        for j, hb in enumerate(tiles):
            P.act(S.junk, S.junk[:], hb, hb[:], AF.Square, extra_writes=[ssb], accum_out=ssb[:, j:j + 1])
        rstd_cols(ssb, len(tiles), D)
        return ssb, 16

    def norm_T(tiles, gcol, dst, hs_l):
        ssb, o = rstd_group(tiles)
        for j, hb in enumerate(tiles):
            hs = rot("hs", hs_l)
            pT = rot("pT", S.bank[6:8])
            P.ts('dve', hs, hs[:], hb, hb[:], ssb[:, o + j:o + j + 1], None, ALU.mult, extra_reads=[ssb])
            for k in range(8):
                P.tr(pT, bf(pT)[:, k * 128:(k + 1) * 128], hs, hs[:, k * 128:(k + 1) * 128], S.ident, S.ident[:],
                     inc=(k == 7))
            P.tt('dve', dst, dst[:, :, j * 128:(j + 1) * 128], pT,
                 bf(pT).rearrange("p (k n) -> p k n", k=8), gcol, gcol[:].to_broadcast([128, 8, 128]), ALU.mult)

    def proj_out_add(srcT, wt, tiles, scale):
        for j, hb in enumerate(tiles):
            for dh in range(2):
                pd = rot("pd", S.bank[4:6])
                for c in range(8):
                    P.mm(pd, pd[:], srcT, srcT[:, c, j * 128:(j + 1) * 128],
                         wt, wt[:, c, dh * 512:(dh + 1) * 512], start=(c == 0), stop=(c == 7))
                P.stt('dve', hb, hb[:, dh * 512:(dh + 1) * 512], pd, pd[:], scale,
                      hb, hb[:, dh * 512:(dh + 1) * 512], ALU.mult, ALU.add)

    def ffn(li, pre):
        wg_d, wu_d, wd_d = W[pre + "_w_gate"], W[pre + "_w_up"], W[pre + "_w_down"]
        gcol = S.gcol[(pre + "_norm", li)]
        wg_v = wg_d[li].rearrange("(k p) n -> p k n", p=128)
        wu_v = wu_d[li].rearrange("(k p) n -> p k n", p=128)
        wd_v = wd_d[li].rearrange("(c p) n -> p c n", p=128)
        with P.scope():
            hnT = P.sb("hnT", [128, 8, 1024], BF16)
            actT = P.sb("actT", [128, 12, 1024], BF16)
            wds = [P.sb("wd%d" % i, [128, 12, D], BF16) for i in range(2)]
            wgs = [P.sb("wg%d" % i, [128, 8, 256], BF16) for i in range(2)]
            wus = [P.sb("wu%d" % i, [128, 8, 256], BF16) for i in range(2)]
            hs_l = [P.sb("hs%d" % i, [128, D], BF16) for i in range(2)]
            sgs = [P.sb("sg%d" % i, [128, 512], F32) for i in range(2)]
            for tg in range(2):
                norm_T(S.h[tg * 8:tg * 8 + 8], gcol, hnT, hs_l)
                for (c0, nch) in ((0, 12), (12, 10)):
                    wd = rot("wd", wds)
                    P.dma('pool', wd, wd[:, 0:nch, :], wd_d, wd_v[:, c0:c0 + nch, :])
                    for sc in range(nch // 2):
                        wg = rot("wg", wgs)
                        wu = rot("wu", wus)
                        col = (c0 + 2 * sc) * 128
                        P.dma('pool', wg, wg[:], wg_d, wg_v[:, :, col:col + 256])
                        P.dma('pool', wu, wu[:], wu_d, wu_v[:, :, col:col + 256])
                        for ch in range(2):
                            for ts_ in range(2):
                                pg = rot("pg", S.bank[0:2])
                                pu = rot("pu", S.bank[2:4])
                                sg = rot("sg", sgs)
                                for k in range(8):
                                    P.mm(pg, pg[:], wg, wg[:, k, ch * 128:(ch + 1) * 128],
                                         hnT, hnT[:, k, ts_ * 512:(ts_ + 1) * 512], start=(k == 0), stop=(k == 7))
                                for k in range(8):
                                    P.mm(pu, pu[:], wu, wu[:, k, ch * 128:(ch + 1) * 128],
                                         hnT, hnT[:, k, ts_ * 512:(ts_ + 1) * 512], start=(k == 0), stop=(k == 7))
                                P.act(sg, sg[:], pg, pg[:], AF.Silu)
                                P.tt('dve', actT, actT[:, 2 * sc + ch, ts_ * 512:(ts_ + 1) * 512],
                                     sg, sg[:], pu, pu[:], ALU.mult)
                    for j in range(8):
                        hb = S.h[tg * 8 + j]
                        for dh in range(2):
                            pd = rot("pd", S.bank[4:6])
                            for c in range(nch):
                                P.mm(pd, pd[:], actT, actT[:, c, j * 128:(j + 1) * 128],
                                     wd, wd[:, c, dh * 512:(dh + 1) * 512], start=(c == 0), stop=(c == nch - 1))
                            P.stt('dve', hb, hb[:, dh * 512:(dh + 1) * 512], pd, pd[:], 0.5,
                                  hb, hb[:, dh * 512:(dh + 1) * 512], ALU.mult, ALU.add)

    def xattn(li):
        wq_v = W["xa_w_q"][li].rearrange("(k p) n -> p k n", p=128)
        wkv_v = W["xa_w_kv"][li].rearrange("(k p) n -> p k n", p=128)
        wo_v = W["xa_w_o"][li].rearrange("(k p) n -> p k n", p=128)
        with P.scope():
            hnT = P.sb("hnT", [128, 8, 1024], BF16)
            qT = P.sb("qT", [128, 8, 1024], BF16)
            oT = P.sb("oT", [128, 8, 1024], BF16)
            wA = P.sb("wA", [128, 8, 1024], BF16)
            wB = P.sb("wB", [128, 8, 1024], BF16)
            mT = P.sb("mT", [128, 8, 256], BF16)
            kT = P.sb("kT", [128, 8, 256], BF16)
            Vm = P.sb("Vm", [128, 2, 1024], BF16)
            mt = [P.sb("mt%d" % i, [128, D], F32) for i in range(2)]
            hs_l = [P.sb("hs%d" % i, [128, D], BF16) for i in range(2)]
            ETs = [P.sb("ET%d" % i, [128, 512], BF16) for i in range(4)]
            rls = [P.sb("rl%d" % i, [128, 512], F32) for i in range(2)]
            for i in range(2):
                P.dma('sp', mt[i], mt[i][:], mem, mem[i * 128:(i + 1) * 128, :])
            norm_T(mt, S.gcol[("xa_mem_norm", li)], mT, hs_l)
            P.dma('pool', wA, wA[:], W["xa_w_kv"], wkv_v[:, :, 0:1024])
            P.dma('pool', wB, wB[:], W["xa_w_kv"], wkv_v[:, :, 1024:2048])
            for c in range(8):
                pk = rot("pg", S.bank[0:2])
                for k in range(8):
                    P.mm(pk, pk[:, 0:256], wA, wA[:, k, c * 128:(c + 1) * 128], mT, mT[:, k, :],
                         start=(k == 0), stop=(k == 7))
                P.copy('dve', kT, kT[:, c, :], pk, pk[:, 0:256])
            for i in range(2):
                for dh in range(2):
                    pv = rot("pu", S.bank[2:4])
                    for k in range(8):
                        P.mm(pv, pv[:], mT, mT[:, k, i * 128:(i + 1) * 128], wB, wB[:, k, dh * 512:(dh + 1) * 512],
                             start=(k == 0), stop=(k == 7))
                    P.copy('dve', Vm, Vm[:, i, dh * 512:(dh + 1) * 512], pv, pv[:])
            P.dma('pool', wA, wA[:], W["xa_w_q"], wq_v)
            P.dma('pool', wB, wB[:], W["xa_w_o"], wo_v)
            for tg in range(2):
                tiles = S.h[tg * 8:tg * 8 + 8]
                norm_T(tiles, S.gcol[("xa_norm", li)], hnT, hs_l)
                for c in range(8):
                    for ts_ in range(2):
                        pq = rot("pg", S.bank[0:2])
                        for k in range(8):
                            P.mm(pq, pq[:], wA, wA[:, k, c * 128:(c + 1) * 128],
                                 hnT, hnT[:, k, ts_ * 512:(ts_ + 1) * 512], start=(k == 0), stop=(k == 7))
                        P.act(qT, qT[:, c, ts_ * 512:(ts_ + 1) * 512], pq, pq[:], AF.Copy)
                for hd in range(4):
                    for ts_ in range(2):
                        ets = []
                        for mc in range(2):
                            pS = rot("pu", S.bank[2:4])
                            for cc in range(2):
                                P.mm(pS, pS[:], kT, kT[:, hd * 2 + cc, mc * 128:(mc + 1) * 128],
                                     qT, qT[:, hd * 2 + cc, ts_ * 512:(ts_ + 1) * 512], start=(cc == 0), stop=(cc == 1))
                            ET = rot("ET", ETs)
                            P.act(ET, ET[:], pS, pS[:], AF.Exp, scale=1.0 / 16.0)
                            ets.append(ET)
                        pL = rot("pd", S.bank[4:6])
                        for mc in range(2):
                            P.mm(pL, pL[:], S.ones, S.ones[:], ets[mc], ets[mc][:], start=(mc == 0), stop=(mc == 1))
                        rl = rot("rl", rls)
                        P.op('dve', lambda g, rl=rl, pL=pL: g.reciprocal(rl[:], pL[:]), reads=[pL], writes=[rl])
                        for vc in range(2):
                            pO = rot("pg", S.bank[0:2])
                            for mc in range(2):
                                P.mm(pO, pO[:], Vm, Vm[:, mc, hd * 256 + vc * 128:hd * 256 + (vc + 1) * 128],
                                     ets[mc], ets[mc][:], start=(mc == 0), stop=(mc == 1))
                            P.tt('dve', oT, oT[:, hd * 2 + vc, ts_ * 512:(ts_ + 1) * 512], pO, pO[:], rl, rl[:], ALU.mult)
                proj_out_add(oT, wB, tiles, 1.0)

    def attn():
        win_v = W["da_w_in"][0].rearrange("(k p) n -> p k n", p=128)
        wout_v = W["da_w_out"][0].rearrange("(k p) n -> p k n", p=128)
        lam_init = 0.8 - 0.6 * float(np.exp(-0.3 * 0))
        with P.scope():
            qT = P.sb("qT", [128, 8, 2048], BF16)
            lam = P.sb("lam", [128, 8], F32)
            gsub = P.sb("gsub", [128, 128], F32)
            with P.scope():
                lt = [P.sb("lt%d" % i, [128, 64], F32) for i in range(4)]
                for i, nm in enumerate(("da_lambda_q1", "da_lambda_k1", "da_lambda_q2", "da_lambda_k2")):
                    P.dma('sp', lt[i], lt[i][:], W[nm], W[nm][0, :].partition_broadcast(128))
                P.tt('dve', lt[0], lt[0][:], lt[0], lt[0][:], lt[1], lt[1][:], ALU.mult)
                P.tt('dve', lt[2], lt[2][:], lt[2], lt[2][:], lt[3], lt[3][:], ALU.mult)
                P.act(lt[1], lt[1][:], lt[0], lt[0][:], AF.Copy, extra_writes=[lam], accum_out=lam[:, 0:1])
                P.act(lt[3], lt[3][:], lt[2], lt[2][:], AF.Copy, extra_writes=[lam], accum_out=lam[:, 1:2])
                P.act(lam, lam[:, 2:4], lam, lam[:, 0:2], AF.Exp)
                P.tt('dve', lam, lam[:, 4:5], lam, lam[:, 2:3], lam, lam[:, 3:4], ALU.subtract)
                P.ts('dve', lam, lam[:, 5:6], lam, lam[:, 4:5], lam_init, -1.0, ALU.add, ALU.mult)
                P.dma('sp', gsub, gsub[:], W["da_subln"], W["da_subln"][0, :].partition_broadcast(128))
                P.ts('dve', gsub, gsub[:], gsub, gsub[:], 1.0 - lam_init, None, ALU.mult)
            with P.scope():
                hnT = P.sb("hnT", [128, 8, 2048], BF16)
                hs_l = [P.sb("hs%d" % i, [128, D], BF16) for i in range(2)]
                cosT = P.sb("cosT", [128, 2048], F32)
                sinT = P.sb("sinT", [128, 2048], F32)
                rc = P.sb("rc", [128, 4], F32)
                P.dma('sp', rc, rc[:], ropec, ropec[:, :])
                with P.scope():
                    pi_ = P.sb("pi", [128, 2048], I32)
                    ang = P.sb("ang", [128, 2048], F32)
                    nf = P.sb("nf", [128, 2048], F32)
                    ni = P.sb("ni", [128, 2048], I32)
                    P.dma('sp', pi_, pi_[:], pos, pos[0, :].partition_broadcast(128))
                    for which, dst in ((0, sinT), (1, cosT)):
                        P.copy('dve', ang, ang[:], pi_, pi_[:])
                        if which == 0:
                            P.ts('dve', ang, ang[:], ang, ang[:], rc[:, 0:1], None, ALU.mult, extra_reads=[rc])
                        else:
                            P.ts('dve', ang, ang[:], ang, ang[:], rc[:, 0:1], float(np.pi / 2), ALU.mult, ALU.add,
                                 extra_reads=[rc])
                        P.ts('dve', nf, nf[:], ang, ang[:], float(1 / (2 * np.pi)), None, ALU.mult)
                        P.copy('dve', ni, ni[:], nf, nf[:])
                        P.copy('dve', nf, nf[:], ni, ni[:])
                        P.stt('dve', ang, ang[:], nf, nf[:], float(-2 * np.pi), ang, ang[:], ALU.mult, ALU.add)
                        P.ts('dve', nf, nf[:], ang, ang[:], float(np.pi), float(2 * np.pi), ALU.is_gt, ALU.mult)
                        P.tt('dve', ang, ang[:], ang, ang[:], nf, nf[:], ALU.subtract)
                        P.ts('dve', ang, ang[:], ang, ang[:], -3.1415925, 3.1415925, ALU.max, ALU.min)
                        if which == 0:
                            P.act(dst, dst[:], ang, ang[:], AF.Sin, extra_reads=[rc], scale=rc[:, 1:2])
                        else:
                            P.act(dst, dst[:], ang, ang[:], AF.Sin)
                norm_T(S.h[0:8], S.gcol[("mix_norm", 0)], hnT[:, :, 0:1024], hs_l) if False else None
                class _V:
                    def __init__(s_, b, off):
                        s_.b, s_.off = b, off
                for tg in range(2):
                    ssb, o = rstd_group(S.h[tg * 8:tg * 8 + 8])
                    for j in range(8):
                        hb = S.h[tg * 8 + j]
                        hs = rot("hs", hs_l)
                        pT = rot("pT", S.bank[6:8])
                        P.ts('dve', hs, hs[:], hb, hb[:], ssb[:, o + j:o + j + 1], None, ALU.mult, extra_reads=[ssb])
                        for k in range(8):
                            P.tr(pT, bf(pT)[:, k * 128:(k + 1) * 128], hs, hs[:, k * 128:(k + 1) * 128],
                                 S.ident, S.ident[:], inc=(k == 7))
                        tcol = (tg * 8 + j) * 128
                        P.tt('dve', hnT, hnT[:, :, tcol:tcol + 128], pT, bf(pT).rearrange("p (k n) -> p k n", k=8),
                             S.gcol[("mix_norm", 0)], S.gcol[("mix_norm", 0)][:].to_broadcast([128, 8, 128]), ALU.mult)
                wn = [P.sb("wn%d" % i, [128, 8, 128], BF16) for i in range(2)]
                wr = [P.sb("wr%d" % i, [128, 8, 128], BF16) for i in range(2)]
                t1s = [P.sb("t1%d" % i, [128, 512], F32) for i in range(2)]
                t2s = [P.sb("t2%d" % i, [128, 512], F32) for i in range(2)]
                kst = [P.sb("kst%d" % i, [128, 2048], BF16) for i in range(1)]
                for c in range(16):
                    w_n = rot("wn", wn)
                    w_r = rot("wr", wr)
                    col = c * 128
                    P.dma('pool', w_n, w_n[:], W["da_w_in"], win_v[:, :, col:col + 128])
                    wn5 = w_n[:].rearrange("p k (b t f) -> p k b t f", b=2, t=2)
                    wr5 = w_r[:].rearrange("p k (b t f) -> p k b t f", b=2, t=2)
                    P.copy('pool', w_r, wr5[:, :, :, 0, :], w_n, wn5[:, :, :, 1, :])
                    P.copy('pool', w_r, wr5[:, :, :, 1, :], w_n, wn5[:, :, :, 0, :])
                    dstb = kst[0] if c >= 8 else qT
                    for tsub in range(4):
                        pn = rot("pg", S.bank[0:2])
                        pr = rot("pu", S.bank[2:4])
                        tsl = slice(tsub * 512, (tsub + 1) * 512)
                        for k in range(8):
                            P.mm(pn, pn[:], w_n, w_n[:, k, :], hnT, hnT[:, k, tsl], start=(k == 0), stop=(k == 7))
                        for k in range(8):
                            P.mm(pr, pr[:], w_r, w_r[:, k, :], hnT, hnT[:, k, tsl], start=(k == 0), stop=(k == 7))
                        t1 = rot("t1", t1s)
                        t2 = rot("t2", t2s)
                        P.tt('dve', t1, t1[:], pn, pn[:], cosT, cosT[:, tsl], ALU.mult)
                        P.tt('dve', t2, t2[:], pr, pr[:], sinT, sinT[:, tsl], ALU.mult)
                        if c < 8:
                            P.tt('dve', t1, t1[:], t1, t1[:], t2, t2[:], ALU.add)
                            P.ts('dve', qT, qT[:, c, tsl], t1, t1[:], 0.125, None, ALU.mult)
                        else:
                            P.tt('dve', dstb, dstb[:, tsl], t1, t1[:], t2, t2[:], ALU.add)
                    if c >= 8:
                        P.dma('sp', kin[c - 8], kin[c - 8][:, :], dstb, dstb[:])
                        P.coll("AllGather", kin[c - 8], kin[c - 8][:, :], kall[c - 8], kall[c - 8][:, :], GROUPS)
                wv = P.sb("wv", [128, 8, 512], BF16)
                vh4 = P.sb("vh4", [128, 4, 16, 129], BF16)
                P.op('dve', lambda g: g.memset(vh4[:].rearrange("p a b c -> p (a b c)"), 1.0), writes=[vh4])
                for dh in range(2):
                    P.dma('pool', wv, wv[:], W["da_w_in"], win_v[:, :, 2048 + dh * 512:2048 + (dh + 1) * 512])
                    for t in range(NT):
                        pv = rot("pd", S.bank[4:6])
                        for k in range(8):
                            P.mm(pv, pv[:], hnT, hnT[:, k, t * 128:(t + 1) * 128], wv, wv[:, k, :],
                                 start=(k == 0), stop=(k == 7))
                        P.act(vh4, vh4[:, :, t, 0:128], pv, pv[:].rearrange("p (h f) -> p h f", h=4), AF.Copy)
                    for hh in range(4):
                        hd = dh * 4 + hh
                        P.dma('sp', vin[hd], vin[hd][:, :], vh4, vh4[:, hh].rearrange("p t j -> p (t j)"))
                        P.coll("AllGather", vin[hd], vin[hd][:, :], vall[hd], vall[hd][:, :], GROUPS)
            if ATTN_STOP == "X":
                return
            on = [P.sb("on%d" % t, [128, D], BF16) for t in range(NT)]
            with P.scope():
                ks = [[P.sb("ks%d_%d" % (i, m), [128, 2048], BF16) for m in range(2)] for i in range(2)]
                for i in range(2):
                    for m in range(2):
                        P.op('dve', lambda g, b=ks[i][m]: g.memset(b[:], 0.0), writes=[ks[i][m]])
                vs_ = [P.sb("vs%d" % i, [128, 16, 136], BF16) for i in range(2)]
                ETs = [P.sb("ET%d" % i, [128, 512], BF16) for i in range(4)]
                of = P.sb("of", [128, 8, 128], F32)
                o1 = P.sb("o1", [128, 128], F32)
                rl = P.sb("rl", [128, 4], F32)
                accb = S.bank[0:6]
                for h in range(ATTN_HEADS):
                    for qh in range(2):
                        started = [False] * 6
                        nsrc = min(ATTN_SRCS, 4)
                        slots = {}

                        def load_src(src, h=h):
                            kb = ks[src % 2]
                            vb = vs_[src % 2]
                            if src == 0:
                                for m in range(2):
                                    P.dma('sp', kb[m], kb[m][m * 64:(m + 1) * 64, :], kin[h], kin[h][m * 64:(m + 1) * 64, :])
                                P.dma('sp', vb, vb[:, :, 0:129], vin[h], vin[h][:, :].rearrange("p (t j) -> p t j", j=129))
                            else:
                                r0 = (src - 1) * 128
                                for m in range(2):
                                    P.dma('sp', kb[m], kb[m][m * 64:(m + 1) * 64, :], kall[h],
                                          kall[h][r0 + m * 64:r0 + (m + 1) * 64, :])
                                P.dma('sp', vb, vb[:, :, 0:129], vall[h], vall[h][r0:r0 + 128, :].rearrange("p (t j) -> p t j", j=129))
                                P.ts('dve', vb, vb[:, :, 0:129], vb, vb[:, :, 0:129], S.cfg[:, 7 + src:8 + src], None, ALU.mult,
                                     extra_reads=[S.cfg])
                            slots[src] = (kb, vb)

                        def emit_pv(u):
                            (src, kt, first, nv, mp, ET) = u
                            kb, vb = slots[src]
                            for i in range(nv):
                                a = mp * 8 + (first + i - qh * 8)
                                bk = accb[a // 3]
                                c0 = (a % 3) * 160
                                st = not started[a // 3]
                                started[a // 3] = True
                                P.mm(bk, bk[:, c0:c0 + 129], ET, ET[:, i * 128:(i + 1) * 128],
                                     vb, vb[:, kt, 0:129], start=st, stop=False, inc=(i == nv - 1),
                                     skip_group_check=True)

                        load_src(0)
                        if nsrc > 1:
                            load_src(1)
                        pending = None
                        for src in range(nsrc):
                            kb, vb = slots[src]
                            first_unit = True
                            for kt in range(16):
                                for qg in range(2):
                                    s0 = qh * 8 + qg * 4
                                    first = max(s0, kt) if src == 0 else s0
                                    nv = s0 + 4 - first
                                    if nv <= 0:
                                        continue
                                    for mp in range(2):
                                        pS = rot("pS", S.bank[6:8])
                                        prow = slice(mp * 64, (mp + 1) * 64)
                                        diag = (src == 0 and first == kt)
                                        P.mm(pS, pS[:, 0:nv * 128], kb[mp], kb[mp][:, kt * 128:(kt + 1) * 128],
                                             qT, qT[:, h, first * 128:(first + nv) * 128],
                                             start=True, stop=not diag)
                                        if diag:
                                            P.mm(pS, pS[:, 0:128], S.ident, S.ident[:], S.negm, S.negm[:],
                                                 start=False, stop=True)
                                        ET = rot("ET", ETs)
                                        P.act(ET, ET[:, 0:nv * 128], pS, pS[:, 0:nv * 128], AF.Exp)
                                        if pending is not None:
                                            emit_pv(pending)
                                        pending = (src, kt, first, nv, mp, ET)
                                        if first_unit:
                                            first_unit = False
                                            if src >= 1 and src + 1 < nsrc:
                                                load_src(src + 1)
                        emit_pv(pending)
                        ssb = rot("ss", S.ss)
                        for qs in range(8):
                            a1, a2 = qs, 8 + qs
                            b1, c1 = accb[a1 // 3], (a1 % 3) * 160
                            b2, c2 = accb[a2 // 3], (a2 % 3) * 160
                            P.op('dve', lambda g, b1=b1, c1=c1: g.reciprocal(rl[:, 0:1], b1[:, c1 + 128:c1 + 129]),
                                 reads=[b1], writes=[rl])
                            P.op('dve', lambda g, b2=b2, c2=c2: g.reciprocal(rl[:, 1:2], b2[:, c2 + 128:c2 + 129]),
                                 reads=[b2], writes=[rl])
                            P.tt('dve', rl, rl[:, 2:3], rl, rl[:, 1:2], lam, lam[:, 5:6], ALU.mult)
                            P.ts('dve', o1, o1[:], b1, b1[:, c1:c1 + 128], rl[:, 0:1], None, ALU.mult, extra_reads=[rl])
                            P.stt('dve', of, of[:, qs, :], b2, b2[:, c2:c2 + 128], rl[:, 2:3], o1, o1[:],
                                  ALU.mult, ALU.add, extra_reads=[rl])
                            P.act(S.junk, S.junk[:, 0:128], of, of[:, qs, :], AF.Square, extra_writes=[ssb],
                                  accum_out=ssb[:, qs:qs + 1])
                        rstd_cols(ssb, 8, 128, lnexp=True)
                        for qs in range(8):
                            ob = on[qh * 8 + qs]
                            P.stt('dve', ob, ob[:, h * 128:(h + 1) * 128], of, of[:, qs, :], ssb[:, 16 + qs:17 + qs],
                                  gsub, gsub[:], ALU.mult, ALU.mult, extra_reads=[ssb])
            with P.scope():
                onT = P.sb("onT", [128, 8, 1024], BF16)
                wo = P.sb("wo", [128, 8, 1024], BF16)
                P.dma('pool', wo, wo[:], W["da_w_out"], wout_v)
                for tg in range(2):
                    for j in range(8):
                        ob = on[tg * 8 + j]
                        pT = rot("pT", S.bank[6:8])
                        for k in range(8):
                            P.tr(pT, bf(pT)[:, k * 128:(k + 1) * 128], ob, ob[:, k * 128:(k + 1) * 128],
                                 S.ident, S.ident[:], inc=(k == 7))
                        P.act(onT, onT[:, :, j * 128:(j + 1) * 128], pT, bf(pT).rearrange("p (k n) -> p k n", k=8), AF.Copy)
                    proj_out_add(onT, wo, S.h[tg * 8:tg * 8 + 8], 1.0)

    def ssd():
        win_v = W["ssd_w_in"][0].rearrange("(k p) n -> p k n", p=128)
        wout_v = W["ssd_w_out"][0].rearrange("(c p) n -> p c n", p=128)
        XB = 2048
        gc = S.gcol[("mix_norm", 1)]
        with P.scope():
            hnT = P.sb("hnT", [128, 8, 2048], BF16)
            hal = P.sb("hal", [128, 8, 4], BF16)
            dt = P.sb("dt", [128, 16, 32], F32)
            a = P.sb("a", [128, 16, 32], F32)
            acat = P.sb("acat", [128, 16, 64], F32)
            eacs = P.sb("eacs", [128, 16, 32], F32)
            dte = P.sb("dte", [128, 16, 32], F32)
            eat = P.sb("eat", [128, 16, 32], F32)
            dtb = P.sb("dtb", [128, 32], F32)
            Ab = P.sb("Ab", [128, 32], F32)
            Db = P.sb("Db", [128, 32], F32)
            Uf = P.sb("Uf", [128, 128], F32)
            Ub = P.sb("Ub", [128, 128], BF16)
            nones = P.sb("nones", [128, 128], BF16)
            Sin = P.sb("Sin", [128, 2048], F32)
            dsum = P.sb("dsum", [128, 32], F32)
            P.dma('sp', Uf, Uf[:], utri_d, utri_d[:, :])
            P.dma('pool', Ub, Ub[:], utri_d, utri_d[:, :])
            P.op('dve', lambda g: g.memset(nones[:], -1.0), writes=[nones])
            P.op('dve', lambda g: g.memset(Sin[:], 0.0), writes=[Sin])
            P.dma('sp', dtb, dtb[:], W["ssd_dt_bias"], W["ssd_dt_bias"][0, :].partition_broadcast(128))
            P.dma('sp', Ab, Ab[:], W["ssd_A_log"], W["ssd_A_log"][0, :].partition_broadcast(128))
            P.dma('sp', Db, Db[:], W["ssd_D"], W["ssd_D"][0, :].partition_broadcast(128))
            P.act(Ab, Ab[:], Ab, Ab[:], AF.Exp)
            P.ts('dve', Ab, Ab[:], Ab, Ab[:], -1.0, None, ALU.mult)
            with P.scope():
                hs_l = [P.sb("hs%d" % i, [128, D], BF16) for i in range(2)]
                for tg in range(2):
                    ssb, o = rstd_group(S.h[tg * 8:tg * 8 + 8])
                    for j in range(8):
                        hb = S.h[tg * 8 + j]
                        hs = rot("hs", hs_l)
                        pT = rot("pT", S.bank[6:8])
                        P.ts('dve', hs, hs[:], hb, hb[:], ssb[:, o + j:o + j + 1], None, ALU.mult, extra_reads=[ssb])
                        for k in range(8):
                            P.tr(pT, bf(pT)[:, k * 128:(k + 1) * 128], hs, hs[:, k * 128:(k + 1) * 128],
                                 S.ident, S.ident[:], inc=(k == 7))
                        tcol = (tg * 8 + j) * 128
                        P.tt('dve', hnT, hnT[:, :, tcol:tcol + 128], pT, bf(pT).rearrange("p (k n) -> p k n", k=8),
                             gc, gc[:].to_broadcast([128, 8, 128]), ALU.mult)
                hr = P.sb("hr", [128, 4, 32], BF16)
                hacc = P.sb("hacc", [128, 32], F32)
                P.dma('sp', halin, halin[:, :].rearrange("p (k t) -> p k t", k=8), hnT, hnT[:, :, 2044:2048])
                P.coll("AllGather", halin, halin[:, :], halall, halall[:, :], GROUPS)
                P.dma('sp', hr, hr[:], halall, halall[:, :].rearrange("(r p) c -> p r c", p=128))
                P.ts('dve', hacc, hacc[:], hr, hr[:, 0, :], S.cfg[:, 4:5], None, ALU.mult, extra_reads=[S.cfg])
                for r in range(1, 4):
                    P.stt('dve', hacc, hacc[:], hr, hr[:, r, :], S.cfg[:, 4 + r:5 + r], hacc, hacc[:], ALU.mult, ALU.add,
                          extra_reads=[S.cfg])
                P.copy('dve', hal, hal[:].rearrange("p k t -> p (k t)"), hacc, hacc[:])
            if SSD_STOP <= 1:
                return
            with P.scope():
                wdt = P.sb("wdt", [128, 8, 32], BF16)
                ahi = P.sb("ahi", [128, 16, 32], BF16)
                alo = P.sb("alo", [128, 16, 32], BF16)
                tmp = P.sb("tmp", [128, 16, 32], F32)
                P.dma('pool', wdt, wdt[:], W["ssd_w_in"], win_v[:, :, 6144:6176])
                for t in range(NT):
                    pd_ = rot("pd", S.bank[4:6])
                    for k in range(8):
                        P.mm(pd_, pd_[:, 0:32], hnT, hnT[:, k, t * 128:(t + 1) * 128], wdt, wdt[:, k, :],
                             start=(k == 0), stop=(k == 7))
                    P.tt('dve', dt, dt[:, t, :], pd_, pd_[:, 0:32], dtb, dtb[:], ALU.add)
                P.act(dt, dt[:], dt, dt[:], AF.Exp)
                P.act(dt, dt[:], dt, dt[:], AF.Ln, bias=1.0)
                P.tt('dve', a, a[:], dt, dt[:], Ab, Ab[:].unsqueeze(1).to_broadcast([128, 16, 32]), ALU.mult)
                P.copy('dve', ahi, ahi[:], a, a[:])
                P.tt('dve', tmp, tmp[:], a, a[:], ahi, ahi[:], ALU.subtract)
                P.copy('dve', alo, alo[:], tmp, tmp[:])
                for t in range(NT):
                    pd_ = rot("pd", S.bank[4:6])
                    P.mm(pd_, pd_[:, 0:32], Ub, Ub[:], ahi, ahi[:, t, :], start=True, stop=False)
                    P.mm(pd_, pd_[:, 0:32], Ub, Ub[:], alo, alo[:, t, :], start=False, stop=True)
                    P.copy('dve', acat, acat[:, t, 0:32], pd_, pd_[:, 0:32])
                    pd_ = rot("pd", S.bank[4:6])
                    P.mm(pd_, pd_[:, 0:32], S.ones, S.ones[:], ahi, ahi[:, t, :], start=True, stop=False)
                    P.mm(pd_, pd_[:, 0:32], S.ones, S.ones[:], alo, alo[:, t, :], start=False, stop=True)
                    P.copy('dve', acat, acat[:, t, 32:64], pd_, pd_[:, 0:32])
                P.act(eacs, eacs[:], acat, acat[:, :, 0:32], AF.Exp)
                P.act(eat, eat[:], acat, acat[:, :, 32:64], AF.Exp)
                P.tt('dve', tmp, tmp[:], acat, acat[:, :, 32:64], acat, acat[:, :, 0:32], ALU.subtract)
                P.act(dte, dte[:], tmp, tmp[:], AF.Exp)
                P.copy('dve', dsum, dsum[:], acat, acat[:, 0, 32:64])
                for t in range(1, NT):
                    P.tt('dve', dsum, dsum[:], dsum, dsum[:], acat, acat[:, t, 32:64], ALU.add)

            if SSD_STOP <= 2:
                return

            def group_pass(g, final):
                with P.scope():
                    xcbc = P.sb("xcbc", [128, 2, 2048], BF16)
                    xs_tok = P.sb("xstok", [128, 16, 256], BF16)
                    B_tok = P.sb("Btok", [128, 16, 128], BF16)
                    prev = P.sb("prev", [128, 256], F32)
                    prevb = P.sb("prevb", [128, 256], BF16)
                    dtd = P.sb("dtd", [128, 16, 4], F32)
                    hs4 = slice(g * 4, g * 4 + 4)
                    cbase = (XB + g * 256, XB + g * 256 + 128, XB + 2048 + g * 128, XB + 3072 + g * 128)
                    with P.scope():
                        wx = P.sb("wx", [128, 8, 512], BF16)
                        cw = P.sb("cw", [128, 4, 4], F32)
                        cb = P.sb("cb", [128, 4], F32)
                        u_l = [P.sb("u%d" % i, [128, 2052], F32) for i in range(2)]
                        acc_l = [P.sb("acc%d" % i, [128, 2048], F32) for i in range(2)]
                        chunks = (0, 1, 2, 3) if final else (0, 1, 2)
                        xcx = P.sb("xcx", [128, 2, 2048], BF16)
                        for ci in chunks:
                            P.dma('pool', wx, wx[:, :, ci * 128:(ci + 1) * 128], W["ssd_w_in"],
                                  win_v[:, :, cbase[ci]:cbase[ci] + 128], accumulate_w=True)
                            cch = cbase[ci] - XB
                            P.dma('sp', cw, cw[:, ci, :], W["ssd_conv_w"],
                                  W["ssd_conv_w"][0][:, cch:cch + 128].rearrange("k p -> p k"), accumulate_w=True)
                            P.dma('sp', cb, cb[:, ci:ci + 1], W["ssd_conv_b"],
                                  W["ssd_conv_b"][0, cch:cch + 128].rearrange("(p o) -> p o", o=1), accumulate_w=True)
                        for ci in chunks:
                            u, acc = u_l[ci % 2], acc_l[ci % 2]
                            ph = rot("pd", S.bank[4:6])
                            for k in range(8):
                                P.mm(ph, ph[:, 0:4], wx, wx[:, k, ci * 128:(ci + 1) * 128], hal, hal[:, k, :],
                                     start=(k == 0), stop=(k == 7))
                            P.act(u, u[:, 0:3], ph, ph[:, 1:4], AF.Copy)
                            for ts_ in range(4):
                                pu_ = rot("pg", S.bank[0:4])
                                for k in range(8):
                                    P.mm(pu_, pu_[:], wx, wx[:, k, ci * 128:(ci + 1) * 128],
                                         hnT, hnT[:, k, ts_ * 512:(ts_ + 1) * 512], start=(k == 0), stop=(k == 7))
                                P.act(u, u[:, 3 + ts_ * 512:3 + (ts_ + 1) * 512], pu_, pu_[:], AF.Copy)
                            P.ts('dve', acc, acc[:], u, u[:, 0:2048], cw[:, ci, 0:1], cb[:, ci:ci + 1], ALU.mult, ALU.add,
                                 extra_reads=[cw, cb])
                            for k in range(1, 4):
                                P.stt('dve', acc, acc[:], u, u[:, k:k + 2048], cw[:, ci, k:k + 1], acc, acc[:],
                                      ALU.mult, ALU.add, extra_reads=[cw])
                            if ci < 2:
                                P.act(xcx, xcx[:, ci, :], acc, acc[:], AF.Silu)
                            else:
                                P.act(xcbc, xcbc[:, ci - 2, :], acc, acc[:], AF.Silu)
                        for t in range(NT if SSD_STOP > 2.2 else 0):
                            pT = rot("pT", S.bank[6:8])
                            tsl_ = slice(t * 128, (t + 1) * 128)
                            for ci in range(3):
                                srcb = xcx if ci < 2 else xcbc
                                P.tr(pT, bf(pT)[:, ci * 128:(ci + 1) * 128], srcb, srcb[:, ci % 2 if ci < 2 else 0, tsl_],
                                     S.ident, S.ident[:], inc=(ci == 2))
                            P.copy('dve', xs_tok, xs_tok[:, t, :], pT, bf(pT)[:, 0:256])
                            P.act(B_tok, B_tok[:, t, :], pT, bf(pT)[:, 256:384], AF.Copy)
                    if SSD_STOP <= 2.5:
                        return
                    xt = P.sb("xt", [128, 16, 256], BF16)
                    xtd = P.sb("xtd", [128, 16, 256], BF16)
                    v4 = lambda ap: ap.rearrange("p t (h f) -> p t h f", h=4)
                    bc4 = lambda ap: ap.unsqueeze(3).to_broadcast([128, 16, 4, 64])
                    P.tt('dve', dtd, dtd[:], dt, dt[:, :, hs4], dte, dte[:, :, hs4], ALU.mult)
                    if final:
                        P.tt('dve', xt, v4(xt[:]), xs_tok, v4(xs_tok[:]), dt, bc4(dt[:, :, hs4]), ALU.mult)
                    P.tt('dve', xtd, v4(xtd[:]), xs_tok, v4(xs_tok[:]), dtd, bc4(dtd[:]), ALU.mult)
                    if SSD_STOP <= 2.7:
                        return
                    if final:
                        P.copy('dve', prev, prev[:], Sin, Sin[:, g * 256:(g + 1) * 256])
                    else:
                        P.op('dve', lambda e: e.memset(prev[:], 0.0), writes=[prev])
                    if final:
                        wz = P.sb("wz", [128, 8, 256], BF16)
                        wo = P.sb("wo", [128, 2, D], BF16)
                        AU_l = [P.sb("AU%d" % i, [128, 4, 128], F32) for i in range(2)]
                        AUh_l = [P.sb("AUh%d" % i, [128, 4, 128], BF16) for i in range(2)]
                        AUl_l = [P.sb("AUl%d" % i, [128, 4, 128], BF16) for i in range(2)]
                        dec_l = [P.sb("dec%d" % i, [128, 4, 128], F32) for i in range(2)]
                        MT_l = [P.sb("MT%d" % i, [128, 4, 128], BF16) for i in range(2)]
                        yb_l = [P.sb("yb%d" % i, [128, 256], F32) for i in range(2)]
                        zs_l = [P.sb("zs%d" % i, [128, 256], F32) for i in range(2)]
                        ynb_l = [P.sb("ynb%d" % i, [128, 256], BF16) for i in range(2)]
                        ynT_l = [P.sb("ynT%d" % i, [128, 2, 128], BF16) for i in range(2)]
                        prevb_l = [P.sb("prevb%d" % i, [128, 256], BF16) for i in range(2)]
                        gng = P.sb("gng", [128, 256], F32)
                        P.dma('sp', gng, gng[:], W["ssd_gnorm"], W["ssd_gnorm"][0, g * 256:(g + 1) * 256].partition_broadcast(128))
                        P.dma('pool', wz, wz[:], W["ssd_w_in"], win_v[:, :, g * 256:(g + 1) * 256])
                        P.dma('pool', wo, wo[:], W["ssd_w_out"], wout_v[:, 2 * g:2 * g + 2, :])
                    y4 = lambda ap: ap.rearrange("p (h f) -> p h f", h=4)
                    b4 = lambda ap: ap.unsqueeze(2).to_broadcast([128, 4, 64])

                    def state_update(c):
                        pS_ = S.bank[5]
                        P.mm(pS_, pS_[:, 0:256], B_tok, B_tok[:, c, :], xtd, xtd[:, c, :], start=True, stop=True)
                        P.tt('dve', prev, y4(prev[:]), prev, y4(prev[:]), eat, b4(eat[:, c, hs4]), ALU.mult)
                        P.tt('dve', prev, prev[:], prev, prev[:], pS_, pS_[:, 0:256], ALU.add)

                    def emit_A(c):
                        csl = slice(c * 128, (c + 1) * 128)
                        i2 = c % 2
                        AU, AUh, AUl, dec, MT = AU_l[i2], AUh_l[i2], AUl_l[i2], dec_l[i2], MT_l[i2]
                        zs, prevb, yA = zs_l[i2], prevb_l[i2], yA_l[i2]
                        P.copy('dve', prevb, prevb[:], prev, prev[:])
                        pCB = S.bank[0]
                        P.mm(pCB, pCB[:, 0:128], xcbc, xcbc[:, 0, csl], xcbc, xcbc[:, 1, csl], start=True, stop=True)
                        P.tt('pool', AU, AU[:], a, a[:, c, hs4].unsqueeze(2).to_broadcast([128, 4, 128]),
                             Uf, Uf[:].unsqueeze(1).to_broadcast([128, 4, 128]), ALU.mult)
                        P.copy('pool', AUh, AUh[:], AU, AU[:])
                        P.tt('pool', AU, AU[:], AU, AU[:], AUh, AUh[:], ALU.subtract)
                        P.copy('pool', AUl, AUl[:], AU, AU[:])
                        pR = S.bank[1]
                        P.mm(pR, pR[:], S.ones, S.ones[:], AUh, AUh[:].rearrange("p h l -> p (h l)"),
                             start=True, stop=False, inc=False)
                        P.mm(pR, pR[:], S.ones, S.ones[:], AUl, AUl[:].rearrange("p h l -> p (h l)"),
                             start=False, stop=False, inc=False)
                        for hh in range(4):
                            cs_ = slice(hh * 128, (hh + 1) * 128)
                            P.mm(pR, pR[:, cs_], AUh, AUh[:, hh, :], nones, nones[:], start=False, stop=False, inc=False)
                            P.mm(pR, pR[:, cs_], AUl, AUl[:, hh, :], nones, nones[:], start=False, stop=False, inc=False)
                            P.mm(pR, pR[:, cs_], S.ident, S.ident[:], S.negm, S.negm[:], start=False, stop=(hh == 3),
                                 inc=(hh == 3))
                        P.act(dec, dec[:].rearrange("p h l -> p (h l)"), pR, pR[:], AF.Exp)
                        P.tt('dve', MT, MT[:], dec, dec[:], pCB, pCB[:, 0:128].unsqueeze(1).to_broadcast([128, 4, 128]),
                             ALU.mult)
                        pY = S.bank[2]
                        for hh in range(4):
                            P.mm(pY, pY[:, hh * 64:(hh + 1) * 64], MT, MT[:, hh, :], xt, xt[:, c, hh * 64:(hh + 1) * 64],
                                 start=(hh == 0), stop=(hh == 3), inc=(hh == 3), skip_group_check=True)
                        P.tt('dve', dx, y4(dx[:]), xs_tok, y4(xs_tok[:, c, :]), Db, b4(Db[:, hs4]), ALU.mult)
                        P.tt('dve', yA, yA[:], dx, dx[:], pY, pY[:, 0:256], ALU.add)
                        pZ = S.bank[4]
                        for k in range(8):
                            P.mm(pZ, pZ[:, 0:256], hnT, hnT[:, k, csl], wz, wz[:, k, :], start=(k == 0), stop=(k == 7))
                        P.act(zs, zs[:], pZ, pZ[:, 0:256], AF.Exp, scale=-1.0)
                        P.act(zs, zs[:], zs, zs[:], AF.Ln, bias=1.0)
                        P.act(zs, zs[:], zs, zs[:], AF.Exp, scale=-1.0)
                        P.tt('dve', zs, zs[:], zs, zs[:], pZ, pZ[:, 0:256], ALU.mult)
                        state_update(c)

                    def emit_B(c):
                        csl = slice(c * 128, (c + 1) * 128)
                        i2 = c % 2
                        zs, prevb, yA = zs_l[i2], prevb_l[i2], yA_l[i2]
                        yb, ynb, ynT = yb_l[i2], ynb_l[i2], ynT_l[i2]
                        pO = S.bank[3]
                        P.mm(pO, pO[:, 0:256], xcbc, xcbc[:, 1, csl], prevb, prevb[:], start=True, stop=True)
                        P.tt('dve', yb, y4(yb[:]), pO, y4(pO[:, 0:256]), eacs, b4(eacs[:, c, hs4]), ALU.mult)
                        P.tt('dve', yb, yb[:], yb, yb[:], yA, yA[:], ALU.add)
                        P.tt('dve', yb, yb[:], yb, yb[:], zs, zs[:], ALU.mult)
                        ssb = rot("ss", S.ss)
                        P.act(S.junk, S.junk[:, 0:256], yb, yb[:], AF.Square, extra_writes=[ssb], accum_out=ssb[:, 0:1])
                        rstd_cols(ssb, 1, 256, lnexp=True)
                        P.stt('dve', ynb, ynb[:], yb, yb[:], ssb[:, 16:17], gng, gng[:],
                              ALU.mult, ALU.mult, extra_reads=[ssb])
                        pT = S.bank[7]
                        for cc in range(2):
                            P.tr(pT, bf(pT)[:, cc * 128:(cc + 1) * 128], ynb, ynb[:, cc * 128:(cc + 1) * 128],
                                 S.ident, S.ident[:], inc=(cc == 1))
                        P.copy('dve', ynT, ynT[:].rearrange("p c n -> p (c n)"), pT, bf(pT)[:, 0:256])
                        hb = S.h[c]
                        for dh in range(2):
                            pd_ = S.bank[6]
                            for cc in range(2):
                                P.mm(pd_, pd_[:], ynT, ynT[:, cc, :], wo, wo[:, cc, dh * 512:(dh + 1) * 512],
                                     start=(cc == 0), stop=(cc == 1))
                            P.tt('dve', hb, hb[:, dh * 512:(dh + 1) * 512], hb, hb[:, dh * 512:(dh + 1) * 512],
                                 pd_, pd_[:], ALU.add)

                    if final:
                        yA_l = [P.sb("yA%d" % i, [128, 256], F32) for i in range(2)]
                        dx = P.sb("dx", [128, 256], F32)
                        for c in range(NT):
                            emit_A(c)
                            if c >= 1:
                                emit_B(c - 1)
                        emit_B(NT - 1)
                    else:
                        for c in range(NT):
                            state_update(c)
                    if not final:
                        P.dma('sp', ssdin, ssdin[:, g * 256:(g + 1) * 256], prev, prev[:], accumulate_w=True)

            for g in range(8 if SSD_STOP > 3 else 1):
                group_pass(g, False)
            if SSD_STOP < 3:
                return
            if SSD_STOP <= 3:
                return
            P.dma('sp', ssdd, ssdd[:, :], dsum, dsum[:])
            P.coll("AllGather", ssdin, ssdin[:, :], ssdall, ssdall[:, :], GROUPS)
            P.coll("AllGather", ssdd, ssdd[:, :], ssddall, ssddall[:, :], GROUPS)
            with P.scope():
                fr = [P.sb("fr%d" % i, [128, 2080], F32) for i in range(2)]
                dm = P.sb("dm", [128, 32], F32)
                for r in range(4):
                    f_ = rot("fr", fr)
                    P.dma('sp', f_, f_[:, 0:2048], ssdall, ssdall[r * 128:(r + 1) * 128, :])
                    P.dma('sp', f_, f_[:, 2048:2080], ssddall, ssddall[r * 128:(r + 1) * 128, :], accumulate_w=True)
                    P.act(dm, dm[:], f_, f_[:, 2048:2080], AF.Exp)
                    P.ts('dve', dm, dm[:], dm, dm[:], S.cfg[:, 8 + r:9 + r], S.cfg[:, 12 + r:13 + r], ALU.mult, ALU.add,
                         extra_reads=[S.cfg])
                    s4 = lambda ap: ap.rearrange("p (h f) -> p h f", h=32)
                    P.tt('dve', Sin, s4(Sin[:]), Sin, s4(Sin[:]), dm, dm[:].unsqueeze(2).to_broadcast([128, 32, 64]), ALU.mult)
                    P.stt('dve', Sin, Sin[:], f_, f_[:, 0:2048], S.cfg[:, 8 + r:9 + r], Sin, Sin[:], ALU.mult, ALU.add,
                          extra_reads=[S.cfg])
            if SSD_STOP <= 4:
                return
            for g in range(8 if SSD_STOP > 5 else 1):
                group_pass(g, True)

    def final_out():
        ssb, o = rstd_group(S.h)
        for t in range(NT):
            hb = S.h[t]
            P.stt('dve', hb, hb[:], hb, hb[:], ssb[:, o + t:o + t + 1], S.gfin, S.gfin[:], ALU.mult, ALU.mult,
                  extra_reads=[ssb])
            P.dma('sp', out, out[t * 128:(t + 1) * 128, :], hb, hb[:], accumulate_w=True)
        P.wait_all('sp', [out])

    for st in stages:
        if st.startswith("ffn"):
            ffn(int(st[-1]), st[:4])
        elif st == "attn":
            attn()
        elif st.startswith("xa"):
            xattn(int(st[-1]))
        elif st == "ssd":
            ssd()
    final_out()
    P.finish()
    es.close()
    nc.used_weights = list(W.keys())
    return nc


def make_in_maps(inputs, used=None):
    x = np.asarray(inputs["x"], dtype=np.float32)
    memv = np.asarray(inputs["mem"], dtype=np.float32)
    posv = np.asarray(inputs["positions"], dtype=np.int32)
    ident = np.eye(128, dtype=np.float32)
    kk, qq = np.meshgrid(np.arange(128), np.arange(128), indexing="ij")
    negmask = np.where(kk > qq, NEG, 0.0).astype(np.float32)
    utri = np.where(kk <= qq, 1.0, 0.0).astype(np.float32)
    inv = (1.0 / (10000.0 ** (np.arange(0, 64, 2, dtype=np.float32) / 64))).astype(np.float32)
    ropec = np.zeros((128, 4), np.float32)
    for p in range(128):
        ropec[p, 0] = inv[p % 32]
        ropec[p, 1] = -1.0 if (p % 64) < 32 else 1.0
    shared = {k: np.ascontiguousarray(np.asarray(inputs[k], dtype=np.float32)) for k, _ in WNAMES
              if used is None or k in used}
    maps = []
    for c in range(NCORES):
        b, q = c // 4, c % 4
        m = dict(shared)
        m["x"] = np.ascontiguousarray(x[b, q * TOK:(q + 1) * TOK, :])
        m["mem"] = np.ascontiguousarray(memv[b])
        m["pos"] = np.ascontiguousarray(posv[b:b + 1, q * TOK:(q + 1) * TOK])
        cfgv = np.zeros((128, 16), np.float32)
        for r in range(4):
            cfgv[:, r] = 0.0 if r < q else NEG
            cfgv[:, 4 + r] = 1.0 if r == q - 1 else 0.0
            cfgv[:, 8 + r] = 1.0 if r < q else 0.0
            cfgv[:, 12 + r] = 0.0 if r < q else 1.0
        m["cfg"] = cfgv
        m["ropec"] = ropec
        m["ident"] = ident
        m["negmask"] = negmask
        m["utri"] = utri
        maps.append(m)
    return maps


def kernel(**inputs):
    nc = build()
    res = run_bass_kernel_spmd(nc, make_in_maps(inputs, nc.used_weights), core_ids=list(range(NCORES)))
    outp = np.empty((2, 8192, D), dtype=np.float32)
    for c in range(NCORES):
        b, q = c // 4, c % 4
        outp[b, q * TOK:(q + 1) * TOK, :] = res.results[c]["out"]
    return outp
```

```python
import numpy as np
from contextlib import ExitStack
import concourse.bass as bass
import concourse.mybir as mybir
from concourse.bass_utils import run_bass_kernel_spmd

F32 = mybir.dt.float32
BF16 = mybir.dt.bfloat16
AF = mybir.ActivationFunctionType
ALU = mybir.AluOpType

NCORES = 8
D = 1024
TOK = 2048
NT = TOK // 128
FH = 2816
EPS = 1e-6
EPOCH = 30000
ENGS = ('pe', 'act', 'dve', 'pool', 'sp')


class Buf:
    __slots__ = ('h', 'name', 'w', 'r', 'dsem', 'dcnt', 'is_dram', 'is_psum')

    def __init__(self, h, name, is_dram=False):
        self.h = h
        self.name = name
        self.is_dram = is_dram
        self.is_psum = False
        self.w = {}
        self.r = {}
        self.dsem = {}
        self.dcnt = {}

    def __getitem__(self, idx):
        return self.h[idx]


class Prog:
    def __init__(self, nc, es):
        self.nc = nc
        self.es = es
        self.q = {e: [] for e in ENGS}
        self.cnt = {e: 0 for e in ENGS}
        self.pending = {e: False for e in ENGS}
        self.waited = {e: {} for e in ENGS}
        self.sems = {}
        self.bufs = []
        self.colls = {}
        self.free_dsems = {}
        self.dtotal = {}
        self.scope_stack = []
        self.root_es = es

    def sb(self, name, shape, dt):
        self.nalloc = getattr(self, 'nalloc', 0) + 1
        name = "%s_%d" % (name, self.nalloc)
        b = self._sb(name, shape, dt)
        self.bufs.append(b)
        if self.scope_stack:
            self.scope_stack[-1].append(b)
        return b

    def _sb(self, name, shape, dt):
        return Buf(self.es.enter_context(self.nc.sbuf_tensor("sb_" + name, list(shape), dt)), "sb_" + name)

    def ps(self, name, shape, dt=F32):
        b = Buf(self.es.enter_context(self.nc.psum_tensor("ps_" + name, list(shape), dt)), "ps_" + name)
        b.is_psum = True
        return b

    def dram(self, name, shape, dt, kind):
        b = Buf(self.nc.dram_tensor(name, list(shape), dt, kind=kind), name, True)
        self.bufs.append(b)
        return b

    def sem(self, key):
        if key not in self.sems:
            self.sems[key] = self.root_es.enter_context(self.nc.semaphore("s%d" % len(self.sems)))
        return self.sems[key]

    def _need(self, e, key, val):
        if key == ('e', 'pe') and e == 'pe':
            return
        if self.waited[e].get(key, 0) >= val:
            return
        self.waited[e][key] = val
        self.q[e].append(('w', key, val))

    def _deps(self, e, reads, writes):
        for b in reads:
            for k, v in b.w.items():
                self._need(e, k, v)
            if b.is_psum:
                for k, v in b.r.items():
                    if k != ('e', e):
                        self._need(e, k, v)
        for b in writes:
            for k, v in b.w.items():
                self._need(e, k, v)
            for k, v in b.r.items():
                self._need(e, k, v)

    def op(self, e, fn, reads=(), writes=(), inc=True):
        self._deps(e, reads, writes)
        key = ('e', e)
        val = self.cnt[e] + 1
        if inc:
            self.cnt[e] = val
            self.pending[e] = False
        else:
            self.pending[e] = True
        self.q[e].append(('i', fn, key if inc else None, 1))
        for b in writes:
            b.w = {key: val}
            b.r = {}
        for b in reads:
            if b.r.get(key, 0) < val:
                b.r[key] = val

    def dma(self, e, ob, out, ib, in_, accumulate_w=False, **kw):
        self._deps(e, [ib], [ob])
        owner = ib if ob.is_dram else ob
        if e not in owner.dsem:
            fl = self.free_dsems.setdefault(e, [])
            if fl:
                owner.dsem[e], owner.dcnt[e] = fl.pop()
            else:
                owner.dsem[e], owner.dcnt[e] = ('d', owner.name + "@" + e), 0
        owner.dcnt[e] += 16
        key, val = owner.dsem[e], owner.dcnt[e]
        self.dtotal[key] = val
        self.q[e].append(('i', lambda eng: eng.dma_start(out=out, in_=in_, **kw), key, 16))
        if accumulate_w:
            ob.w[key] = val
        else:
            ob.w = {key: val}
            ob.r = {}
        ib.r[key] = val

    def wait_all(self, e, bufs):
        self._deps(e, bufs, bufs)

    def barrier(self):
        for e in ENGS:
            for f in ENGS:
                if self.cnt[f] > 0:
                    self._need(e, ('e', f), self.cnt[f])
            for b in self.bufs:
                for e2, k2 in b.dsem.items():
                    self._need(e, k2, b.dcnt[e2])
            for k, v in self.colls.items():
                self._need(e, k, v)

    def scope(self):
        prog = self

        class _Scope:
            def __enter__(self_):
                self_.old = prog.es
                prog.es = ExitStack()
                prog.scope_stack.append([])
                return self_

            def __exit__(self_, *a):
                prog.barrier()
                for b in prog.scope_stack.pop():
                    for e2, k2 in b.dsem.items():
                        prog.free_dsems.setdefault(e2, []).append((k2, b.dcnt[e2]))
                    b.dsem = {}
                    b.dcnt = {}
                prog.es.close()
                prog.es = self_.old
                return False
        return _Scope()

    def coll(self, kind, ib, in_ap, ob, out_ap, groups):
        for k in list(ib.w.keys()):
            if k[0] == 'd':
                ib.w[k] = max(ib.w[k], self.dtotal.get(k, 0))
        self._deps('pool', [ib], [ob])
        key = ('c', len(self.colls))
        self.colls[key] = 1
        self.q['pool'].append(('i', lambda eng: eng.collective_compute(kind, ALU.bypass, replica_groups=groups,
                                                                       ins=[in_ap], outs=[out_ap]), key, 1))
        ob.w = {key: 1}
        ob.r = {}
        ib.r[key] = 1

    def _semh(self, key, val):
        if key[0] == 'c':
            return self.sem((key, 0)), val
        if key[0] == 'e':
            ep = (val - 1) // EPOCH
            return self.sem((key, ep)), val - ep * EPOCH
        ep = (val - 1) // (EPOCH * 16)
        return self.sem((key, ep)), val - ep * EPOCH * 16

    def emit(self, e, eng):
        for item in self.q[e]:
            if item[0] == 'w':
                s, v = self._semh(item[1], item[2])
                eng.wait_ge(s, v)
            else:
                _, fn, key, step = item
                ins = fn(eng)
                if key is not None:
                    if key[0] == 'c':
                        ins.then_inc(self.sem((key, 0)))
                        continue
                    if key[0] == 'e':
                        self._emit_cnt[e] += 1
                        s, _ = self._semh(key, self._emit_cnt[e])
                    else:
                        self._emit_dcnt[key] = self._emit_dcnt.get(key, 0) + 16
                        s, _ = self._semh(key, self._emit_dcnt[key])
                    ins.then_inc(s, step)

    def finish(self):
        nc = self.nc
        for e in ENGS:
            assert not self.pending[e], "engine %s has trailing un-inc'd instructions" % e
        self._emit_cnt = {e: 0 for e in ENGS}
        self._emit_dcnt = {}
        for e in ENGS:
            for item in self.q[e]:
                if item[0] == 'w':
                    self._semh(item[1], item[2])
        with nc.Block() as block:
            @block.sync
            def _(eng):
                self.emit('sp', eng)

            @block.scalar
            def _(eng):
                self.emit('act', eng)

            @block.vector
            def _(eng):
                self.emit('dve', eng)

            @block.gpsimd
            def _(eng):
                self.emit('pool', eng)

            @block.tensor
            def _(eng):
                self.emit('pe', eng)

    def mm(self, ob, out, lb, lhsT, rb, rhs, start=True, stop=True, inc=None, **kw):
        self.op('pe', lambda e: e.matmul(out, lhsT, rhs, start=start, stop=stop, **kw),
                reads=[lb, rb], writes=[ob], inc=stop if inc is None else inc)

    def tr(self, ob, out, ib, in_, identb, ident, inc=True):
        self.op('pe', lambda e: e.transpose(out, in_, ident), reads=[ib, identb], writes=[ob], inc=inc)

    def act(self, ob, out, ib, in_, func, extra_reads=(), extra_writes=(), **kw):
        self.op('act', lambda e: e.activation(out, in_, func, **kw),
                reads=[ib] + list(extra_reads), writes=[ob] + list(extra_writes))

    def ts(self, e, ob, out, ib, in0, s1, s2, op0, op1=None, extra_reads=()):
        if op1 is None:
            self.op(e, lambda g: g.tensor_scalar(out, in0, s1, s2, op0), reads=[ib] + list(extra_reads), writes=[ob])
        else:
            self.op(e, lambda g: g.tensor_scalar(out, in0, s1, s2, op0, op1), reads=[ib] + list(extra_reads), writes=[ob])

    def tt(self, e, ob, out, ab, in0, bb, in1, op):
        self.op(e, lambda g: g.tensor_tensor(out, in0, in1, op), reads=[ab, bb], writes=[ob])

    def stt(self, e, ob, out, ab, in0, scalar, bb, in1, op0, op1, extra_reads=()):
        self.op(e, lambda g: g.scalar_tensor_tensor(out, in0, scalar, in1, op0, op1),
                reads=[ab, bb] + list(extra_reads), writes=[ob])

    def copy(self, e, ob, out, ib, in_):
        self.op(e, lambda g: g.tensor_copy(out, in_), reads=[ib], writes=[ob])


class St:
    pass


I32 = mybir.dt.int32
NEG = -30000.0
GROUPS = [[0, 1, 2, 3], [4, 5, 6, 7]]
ATTN_STOP = None
ATTN_HEADS = 8
ATTN_SRCS = 5
SSD_STOP = 99
ATTN_NOBIAS = True
WNAMES = (("ffn1_norm", [2, D]), ("ffn1_w_gate", [2, D, FH]), ("ffn1_w_up", [2, D, FH]), ("ffn1_w_down", [2, FH, D]),
          ("mix_norm", [2, D]), ("da_w_in", [1, D, 3072]), ("da_lambda_q1", [1, 64]), ("da_lambda_k1", [1, 64]),
          ("da_lambda_q2", [1, 64]), ("da_lambda_k2", [1, 64]), ("da_subln", [1, 128]), ("da_w_out", [1, D, D]),
          ("ssd_w_in", [1, D, 6176]), ("ssd_conv_w", [1, 4, 4096]), ("ssd_conv_b", [1, 4096]), ("ssd_dt_bias", [1, 32]),
          ("ssd_A_log", [1, 32]), ("ssd_D", [1, 32]), ("ssd_gnorm", [1, 2048]), ("ssd_w_out", [1, 2048, D]),
          ("xa_norm", [2, D]), ("xa_mem_norm", [2, D]), ("xa_w_q", [2, D, D]), ("xa_w_kv", [2, D, 2 * D]),
          ("xa_w_o", [2, D, D]),
          ("ffn2_norm", [2, D]), ("ffn2_w_gate", [2, D, FH]), ("ffn2_w_up", [2, D, FH]), ("ffn2_w_down", [2, FH, D]),
          ("final_norm", [D]))


def colvec(ap1d):
    return ap1d.rearrange("(k p o) -> p k o", p=128, o=1)


def build(stages=("ffn1_0", "attn", "xa_0", "ffn2_0", "ffn1_1", "ssd", "xa_1", "ffn2_1")):
    nc = bass.Bass("TRN2", target_bir_lowering=False)
    es = ExitStack()
    es.enter_context(nc.allow_low_precision("bf16 matmul operands, fp32 accumulation"))
    es.enter_context(nc.allow_non_contiguous_dma("small parameter vectors"))
    P = Prog(nc, es)
    S = St()

    def din(name, shape, dt=F32):
        return P.dram(name, shape, dt, "ExternalInput")

    x = din("x", [TOK, D])
    mem = din("mem", [256, D])
    pos = din("pos", [1, TOK], I32)
    cfg = din("cfg", [128, 16])
    utri_d = din("utri", [128, 128])
    halin = P.dram("halin", [128, 32], BF16, "Internal")
    halall = P.dram("halall", [512, 32], BF16, "Internal")
    ssdin = P.dram("ssdin", [128, 2048], F32, "Internal")
    ssdall = P.dram("ssdall", [512, 2048], F32, "Internal")
    ssdd = P.dram("ssdd", [128, 32], F32, "Internal")
    ssddall = P.dram("ssddall", [512, 32], F32, "Internal")
    ropec = din("ropec", [128, 4])
    ident_d = din("ident", [128, 128])
    negm_d = din("negmask", [128, 128])
    class _LazyW(dict):
        def __missing__(self, nm):
            self[nm] = din(nm, dict(WNAMES)[nm])
            return self[nm]
    W = _LazyW()
    out = P.dram("out", [TOK, D], F32, "ExternalOutput")
    VC = 16 * 129
    kin = [P.dram("kin%d" % i, [128, 2048], BF16, "Internal") for i in range(8)]
    vin = [P.dram("vin%d" % i, [128, VC], BF16, "Internal") for i in range(8)]
    kall = [P.dram("kall%d" % i, [512, 2048], BF16, "Internal") for i in range(8)]
    vall = [P.dram("vall%d" % i, [512, VC], BF16, "Internal") for i in range(8)]

    S.h = [P.sb("h%d" % t, [128, D], F32) for t in range(NT)]
    S.ident = P.sb("ident", [128, 128], BF16)
    S.ones = P.sb("ones", [128, 128], BF16)
    S.negm = P.sb("negm", [128, 128], BF16)
    S.cfg = P.sb("cfg", [128, 16], F32)
    S.junk = P.sb("junk", [128, D], BF16)
    S.ss = [P.sb("ss%d" % i, [128, 32], F32) for i in range(2)]
    S.gcol = {}
    for nm in ("ffn1_norm", "ffn2_norm", "mix_norm", "xa_norm", "xa_mem_norm"):
        for li in range(2):
            S.gcol[(nm, li)] = P.sb("gc_%s%d" % (nm, li), [128, 8, 1], F32)
    S.gfin = P.sb("gfin", [128, D], F32)
    S.bank = [P.ps("bank%d" % i, [128, 512]) for i in range(8)]
    S.rr = {}

    def rot(name, lst):
        i = S.rr.get(name, 0)
        S.rr[name] = i + 1
        return lst[i % len(lst)]

    def bf(bank):
        return bank[:].bitcast(BF16)

    P.dma('pool', S.ident, S.ident[:], ident_d, ident_d[:, :])
    P.dma('pool', S.negm, S.negm[:], negm_d, negm_d[:, :])
    P.dma('sp', S.cfg, S.cfg[:], cfg, cfg[:, :])
    P.op('dve', lambda g: g.memset(S.ones[:], 1.0), writes=[S.ones])
    for (nm, li), b in S.gcol.items():
        P.dma('sp', b, b[:], W[nm], colvec(W[nm][li, :]))
    P.dma('sp', S.gfin, S.gfin[:], W["final_norm"], W["final_norm"][:].partition_broadcast(128))
    for t in range(NT):
        P.dma('sp', S.h[t], S.h[t][:], x, x[t * 128:(t + 1) * 128, :])

    def rstd_cols(ssb, n, dim, lnexp=False):
        P.ts('dve', ssb, ssb[:, 16:16 + n], ssb, ssb[:, 0:n], 1.0 / dim, EPS, ALU.mult, ALU.add)
        if lnexp:
            P.act(ssb, ssb[:, 0:n], ssb, ssb[:, 16:16 + n], AF.Ln)
            P.act(ssb, ssb[:, 16:16 + n], ssb, ssb[:, 0:n], AF.Exp, scale=-0.5)
            return
        P.act(ssb, ssb[:, 0:n], ssb, ssb[:, 16:16 + n], AF.Sqrt)
        P.op('dve', lambda g: g.reciprocal(ssb[:, 16:16 + n], ssb[:, 0:n]), reads=[ssb], writes=[ssb])

    def rstd_group(tiles):
        ssb = rot("ss", S.ss)
        for j, hb in enumerate(tiles):
            P.act(S.junk, S.junk[:], hb, hb[:], AF.Square, extra_writes=[ssb], accum_out=ssb[:, j:j + 1])
        rstd_cols(ssb, len(tiles), D)
        return ssb, 16

    def norm_T(tiles, gcol, dst, hs_l):
        ssb, o = rstd_group(tiles)
        for j, hb in enumerate(tiles):
            hs = rot("hs", hs_l)
            pT = rot("pT", S.bank[6:8])
            P.ts('dve', hs, hs[:], hb, hb[:], ssb[:, o + j:o + j + 1], None, ALU.mult, extra_reads=[ssb])
            for k in range(8):
                P.tr(pT, bf(pT)[:, k * 128:(k + 1) * 128], hs, hs[:, k * 128:(k + 1) * 128], S.ident, S.ident[:],
                     inc=(k == 7))
            P.tt('dve', dst, dst[:, :, j * 128:(j + 1) * 128], pT,
                 bf(pT).rearrange("p (k n) -> p k n", k=8), gcol, gcol[:].to_broadcast([128, 8, 128]), ALU.mult)

    def proj_out_add(srcT, wt, tiles, scale):
        for j, hb in enumerate(tiles):
            for dh in range(2):
                pd = rot("pd", S.bank[4:6])
                for c in range(8):
                    P.mm(pd, pd[:], srcT, srcT[:, c, j * 128:(j + 1) * 128],
                         wt, wt[:, c, dh * 512:(dh + 1) * 512], start=(c == 0), stop=(c == 7))
                P.stt('dve', hb, hb[:, dh * 512:(dh + 1) * 512], pd, pd[:], scale,
                      hb, hb[:, dh * 512:(dh + 1) * 512], ALU.mult, ALU.add)

    def ffn(li, pre):
        wg_d, wu_d, wd_d = W[pre + "_w_gate"], W[pre + "_w_up"], W[pre + "_w_down"]
        gcol = S.gcol[(pre + "_norm", li)]
        wg_v = wg_d[li].rearrange("(k p) n -> p k n", p=128)
        wu_v = wu_d[li].rearrange("(k p) n -> p k n", p=128)
        wd_v = wd_d[li].rearrange("(c p) n -> p c n", p=128)
        with P.scope():
            hnT = P.sb("hnT", [128, 8, 1024], BF16)
            actT = P.sb("actT", [128, 12, 1024], BF16)
            wds = [P.sb("wd%d" % i, [128, 12, D], BF16) for i in range(2)]
            wgs = [P.sb("wg%d" % i, [128, 8, 256], BF16) for i in range(2)]
            wus = [P.sb("wu%d" % i, [128, 8, 256], BF16) for i in range(2)]
            hs_l = [P.sb("hs%d" % i, [128, D], BF16) for i in range(2)]
            sgs = [P.sb("sg%d" % i, [128, 512], F32) for i in range(2)]
            for tg in range(2):
                norm_T(S.h[tg * 8:tg * 8 + 8], gcol, hnT, hs_l)
                for (c0, nch) in ((0, 12), (12, 10)):
                    wd = rot("wd", wds)
                    P.dma('pool', wd, wd[:, 0:nch, :], wd_d, wd_v[:, c0:c0 + nch, :])
                    for sc in range(nch // 2):
                        wg = rot("wg", wgs)
                        wu = rot("wu", wus)
                        col = (c0 + 2 * sc) * 128
                        P.dma('pool', wg, wg[:], wg_d, wg_v[:, :, col:col + 256])
                        P.dma('pool', wu, wu[:], wu_d, wu_v[:, :, col:col + 256])
                        for ch in range(2):
                            for ts_ in range(2):
                                pg = rot("pg", S.bank[0:2])
                                pu = rot("pu", S.bank[2:4])
                                sg = rot("sg", sgs)
                                for k in range(8):
                                    P.mm(pg, pg[:], wg, wg[:, k, ch * 128:(ch + 1) * 128],
                                         hnT, hnT[:, k, ts_ * 512:(ts_ + 1) * 512], start=(k == 0), stop=(k == 7))
                                for k in range(8):
                                    P.mm(pu, pu[:], wu, wu[:, k, ch * 128:(ch + 1) * 128],
                                         hnT, hnT[:, k, ts_ * 512:(ts_ + 1) * 512], start=(k == 0), stop=(k == 7))
                                P.act(sg, sg[:], pg, pg[:], AF.Silu)
                                P.tt('dve', actT, actT[:, 2 * sc + ch, ts_ * 512:(ts_ + 1) * 512],
                                     sg, sg[:], pu, pu[:], ALU.mult)
                    for j in range(8):
                        hb = S.h[tg * 8 + j]
                        for dh in range(2):
                            pd = rot("pd", S.bank[4:6])
                            for c in range(nch):
                                P.mm(pd, pd[:], actT, actT[:, c, j * 128:(j + 1) * 128],
                                     wd, wd[:, c, dh * 512:(dh + 1) * 512], start=(c == 0), stop=(c == nch - 1))
                            P.stt('dve', hb, hb[:, dh * 512:(dh + 1) * 512], pd, pd[:], 0.5,
                                  hb, hb[:, dh * 512:(dh + 1) * 512], ALU.mult, ALU.add)

    def xattn(li):
        wq_v = W["xa_w_q"][li].rearrange("(k p) n -> p k n", p=128)
        wkv_v = W["xa_w_kv"][li].rearrange("(k p) n -> p k n", p=128)
        wo_v = W["xa_w_o"][li].rearrange("(k p) n -> p k n", p=128)
        with P.scope():
            hnT = P.sb("hnT", [128, 8, 1024], BF16)
            qT = P.sb("qT", [128, 8, 1024], BF16)
            oT = P.sb("oT", [128, 8, 1024], BF16)
            wA = P.sb("wA", [128, 8, 1024], BF16)
            wB = P.sb("wB", [128, 8, 1024], BF16)
            mT = P.sb("mT", [128, 8, 256], BF16)
            kT = P.sb("kT", [128, 8, 256], BF16)
            Vm = P.sb("Vm", [128, 2, 1024], BF16)
            mt = [P.sb("mt%d" % i, [128, D], F32) for i in range(2)]
            hs_l = [P.sb("hs%d" % i, [128, D], BF16) for i in range(2)]
            ETs = [P.sb("ET%d" % i, [128, 512], BF16) for i in range(4)]
            rls = [P.sb("rl%d" % i, [128, 512], F32) for i in range(2)]
            for i in range(2):
                P.dma('sp', mt[i], mt[i][:], mem, mem[i * 128:(i + 1) * 128, :])
            norm_T(mt, S.gcol[("xa_mem_norm", li)], mT, hs_l)
            P.dma('pool', wA, wA[:], W["xa_w_kv"], wkv_v[:, :, 0:1024])
            P.dma('pool', wB, wB[:], W["xa_w_kv"], wkv_v[:, :, 1024:2048])
            for c in range(8):
                pk = rot("pg", S.bank[0:2])
                for k in range(8):
                    P.mm(pk, pk[:, 0:256], wA, wA[:, k, c * 128:(c + 1) * 128], mT, mT[:, k, :],
                         start=(k == 0), stop=(k == 7))
                P.copy('dve', kT, kT[:, c, :], pk, pk[:, 0:256])
            for i in range(2):
                for dh in range(2):
                    pv = rot("pu", S.bank[2:4])
                    for k in range(8):
                        P.mm(pv, pv[:], mT, mT[:, k, i * 128:(i + 1) * 128], wB, wB[:, k, dh * 512:(dh + 1) * 512],
                             start=(k == 0), stop=(k == 7))
                    P.copy('dve', Vm, Vm[:, i, dh * 512:(dh + 1) * 512], pv, pv[:])
            P.dma('pool', wA, wA[:], W["xa_w_q"], wq_v)
            P.dma('pool', wB, wB[:], W["xa_w_o"], wo_v)
            for tg in range(2):
                tiles = S.h[tg * 8:tg * 8 + 8]
                norm_T(tiles, S.gcol[("xa_norm", li)], hnT, hs_l)
                for c in range(8):
                    for ts_ in range(2):
                        pq = rot("pg", S.bank[0:2])
                        for k in range(8):
                            P.mm(pq, pq[:], wA, wA[:, k, c * 128:(c + 1) * 128],
                                 hnT, hnT[:, k, ts_ * 512:(ts_ + 1) * 512], start=(k == 0), stop=(k == 7))
                        P.act(qT, qT[:, c, ts_ * 512:(ts_ + 1) * 512], pq, pq[:], AF.Copy)
                for hd in range(4):
                    for ts_ in range(2):
                        ets = []
                        for mc in range(2):
                            pS = rot("pu", S.bank[2:4])
                            for cc in range(2):
                                P.mm(pS, pS[:], kT, kT[:, hd * 2 + cc, mc * 128:(mc + 1) * 128],
                                     qT, qT[:, hd * 2 + cc, ts_ * 512:(ts_ + 1) * 512], start=(cc == 0), stop=(cc == 1))
                            ET = rot("ET", ETs)
                            P.act(ET, ET[:], pS, pS[:], AF.Exp, scale=1.0 / 16.0)
                            ets.append(ET)
                        pL = rot("pd", S.bank[4:6])
                        for mc in range(2):
                            P.mm(pL, pL[:], S.ones, S.ones[:], ets[mc], ets[mc][:], start=(mc == 0), stop=(mc == 1))
                        rl = rot("rl", rls)
                        P.op('dve', lambda g, rl=rl, pL=pL: g.reciprocal(rl[:], pL[:]), reads=[pL], writes=[rl])
                        for vc in range(2):
                            pO = rot("pg", S.bank[0:2])
                            for mc in range(2):
                                P.mm(pO, pO[:], Vm, Vm[:, mc, hd * 256 + vc * 128:hd * 256 + (vc + 1) * 128],
                                     ets[mc], ets[mc][:], start=(mc == 0), stop=(mc == 1))
                            P.tt('dve', oT, oT[:, hd * 2 + vc, ts_ * 512:(ts_ + 1) * 512], pO, pO[:], rl, rl[:], ALU.mult)
                proj_out_add(oT, wB, tiles, 1.0)

    def attn():
        win_v = W["da_w_in"][0].rearrange("(k p) n -> p k n", p=128)
        wout_v = W["da_w_out"][0].rearrange("(k p) n -> p k n", p=128)
        lam_init = 0.8 - 0.6 * float(np.exp(-0.3 * 0))
        with P.scope():
            qT = P.sb("qT", [128, 8, 2048], BF16)
            lam = P.sb("lam", [128, 8], F32)
            gsub = P.sb("gsub", [128, 128], F32)
            with P.scope():
                lt = [P.sb("lt%d" % i, [128, 64], F32) for i in range(4)]
                for i, nm in enumerate(("da_lambda_q1", "da_lambda_k1", "da_lambda_q2", "da_lambda_k2")):
                    P.dma('sp', lt[i], lt[i][:], W[nm], W[nm][0, :].partition_broadcast(128))
                P.tt('dve', lt[0], lt[0][:], lt[0], lt[0][:], lt[1], lt[1][:], ALU.mult)
                P.tt('dve', lt[2], lt[2][:], lt[2], lt[2][:], lt[3], lt[3][:], ALU.mult)
                P.act(lt[1], lt[1][:], lt[0], lt[0][:], AF.Copy, extra_writes=[lam], accum_out=lam[:, 0:1])
                P.act(lt[3], lt[3][:], lt[2], lt[2][:], AF.Copy, extra_writes=[lam], accum_out=lam[:, 1:2])
                P.act(lam, lam[:, 2:4], lam, lam[:, 0:2], AF.Exp)
                P.tt('dve', lam, lam[:, 4:5], lam, lam[:, 2:3], lam, lam[:, 3:4], ALU.subtract)
                P.ts('dve', lam, lam[:, 5:6], lam, lam[:, 4:5], lam_init, -1.0, ALU.add, ALU.mult)
                P.dma('sp', gsub, gsub[:], W["da_subln"], W["da_subln"][0, :].partition_broadcast(128))
                P.ts('dve', gsub, gsub[:], gsub, gsub[:], 1.0 - lam_init, None, ALU.mult)
            with P.scope():
                hnT = P.sb("hnT", [128, 8, 2048], BF16)
                hs_l = [P.sb("hs%d" % i, [128, D], BF16) for i in range(2)]
                cosT = P.sb("cosT", [128, 2048], F32)
                sinT = P.sb("sinT", [128, 2048], F32)
                rc = P.sb("rc", [128, 4], F32)
                P.dma('sp', rc, rc[:], ropec, ropec[:, :])
                with P.scope():
                    pi_ = P.sb("pi", [128, 2048], I32)
                    ang = P.sb("ang", [128, 2048], F32)
                    nf = P.sb("nf", [128, 2048], F32)
                    ni = P.sb("ni", [128, 2048], I32)
                    P.dma('sp', pi_, pi_[:], pos, pos[0, :].partition_broadcast(128))
                    for which, dst in ((0, sinT), (1, cosT)):
                        P.copy('dve', ang, ang[:], pi_, pi_[:])
                        if which == 0:
                            P.ts('dve', ang, ang[:], ang, ang[:], rc[:, 0:1], None, ALU.mult, extra_reads=[rc])
                        else:
                            P.ts('dve', ang, ang[:], ang, ang[:], rc[:, 0:1], float(np.pi / 2), ALU.mult, ALU.add,
                                 extra_reads=[rc])
                        P.ts('dve', nf, nf[:], ang, ang[:], float(1 / (2 * np.pi)), None, ALU.mult)
                        P.copy('dve', ni, ni[:], nf, nf[:])
                        P.copy('dve', nf, nf[:], ni, ni[:])
                        P.stt('dve', ang, ang[:], nf, nf[:], float(-2 * np.pi), ang, ang[:], ALU.mult, ALU.add)
                        P.ts('dve', nf, nf[:], ang, ang[:], float(np.pi), float(2 * np.pi), ALU.is_gt, ALU.mult)
                        P.tt('dve', ang, ang[:], ang, ang[:], nf, nf[:], ALU.subtract)
                        P.ts('dve', ang, ang[:], ang, ang[:], -3.1415925, 3.1415925, ALU.max, ALU.min)
                        if which == 0:
                            P.act(dst, dst[:], ang, ang[:], AF.Sin, extra_reads=[rc], scale=rc[:, 1:2])
                        else:
                            P.act(dst, dst[:], ang, ang[:], AF.Sin)
                norm_T(S.h[0:8], S.gcol[("mix_norm", 0)], hnT[:, :, 0:1024], hs_l) if False else None
                class _V:
                    def __init__(s_, b, off):
                        s_.b, s_.off = b, off
                for tg in range(2):
                    ssb, o = rstd_group(S.h[tg * 8:tg * 8 + 8])
                    for j in range(8):
                        hb = S.h[tg * 8 + j]
                        hs = rot("hs", hs_l)
                        pT = rot("pT", S.bank[6:8])
                        P.ts('dve', hs, hs[:], hb, hb[:], ssb[:, o + j:o + j + 1], None, ALU.mult, extra_reads=[ssb])
                        for k in range(8):
                            P.tr(pT, bf(pT)[:, k * 128:(k + 1) * 128], hs, hs[:, k * 128:(k + 1) * 128],
                                 S.ident, S.ident[:], inc=(k == 7))
                        tcol = (tg * 8 + j) * 128
                        P.tt('dve', hnT, hnT[:, :, tcol:tcol + 128], pT, bf(pT).rearrange("p (k n) -> p k n", k=8),
                             S.gcol[("mix_norm", 0)], S.gcol[("mix_norm", 0)][:].to_broadcast([128, 8, 128]), ALU.mult)
                wn = [P.sb("wn%d" % i, [128, 8, 128], BF16) for i in range(2)]
                wr = [P.sb("wr%d" % i, [128, 8, 128], BF16) for i in range(2)]
                t1s = [P.sb("t1%d" % i, [128, 512], F32) for i in range(2)]
                t2s = [P.sb("t2%d" % i, [128, 512], F32) for i in range(2)]
                kst = [P.sb("kst%d" % i, [128, 2048], BF16) for i in range(1)]
                for c in range(16):
                    w_n = rot("wn", wn)
                    w_r = rot("wr", wr)
                    col = c * 128
                    P.dma('pool', w_n, w_n[:], W["da_w_in"], win_v[:, :, col:col + 128])
                    wn5 = w_n[:].rearrange("p k (b t f) -> p k b t f", b=2, t=2)
                    wr5 = w_r[:].rearrange("p k (b t f) -> p k b t f", b=2, t=2)
                    P.copy('pool', w_r, wr5[:, :, :, 0, :], w_n, wn5[:, :, :, 1, :])
                    P.copy('pool', w_r, wr5[:, :, :, 1, :], w_n, wn5[:, :, :, 0, :])
                    dstb = kst[0] if c >= 8 else qT
                    for tsub in range(4):
                        pn = rot("pg", S.bank[0:2])
                        pr = rot("pu", S.bank[2:4])
                        tsl = slice(tsub * 512, (tsub + 1) * 512)
                        for k in range(8):
                            P.mm(pn, pn[:], w_n, w_n[:, k, :], hnT, hnT[:, k, tsl], start=(k == 0), stop=(k == 7))
                        for k in range(8):
                            P.mm(pr, pr[:], w_r, w_r[:, k, :], hnT, hnT[:, k, tsl], start=(k == 0), stop=(k == 7))
                        t1 = rot("t1", t1s)
                        t2 = rot("t2", t2s)
                        P.tt('dve', t1, t1[:], pn, pn[:], cosT, cosT[:, tsl], ALU.mult)
                        P.tt('dve', t2, t2[:], pr, pr[:], sinT, sinT[:, tsl], ALU.mult)
                        if c < 8:
                            P.tt('dve', t1, t1[:], t1, t1[:], t2, t2[:], ALU.add)
                            P.ts('dve', qT, qT[:, c, tsl], t1, t1[:], 0.125, None, ALU.mult)
                        else:
                            P.tt('dve', dstb, dstb[:, tsl], t1, t1[:], t2, t2[:], ALU.add)
                    if c >= 8:
                        P.dma('sp', kin[c - 8], kin[c - 8][:, :], dstb, dstb[:])
                        P.coll("AllGather", kin[c - 8], kin[c - 8][:, :], kall[c - 8], kall[c - 8][:, :], GROUPS)
                wv = P.sb("wv", [128, 8, 512], BF16)
                vh4 = P.sb("vh4", [128, 4, 16, 129], BF16)
                P.op('dve', lambda g: g.memset(vh4[:].rearrange("p a b c -> p (a b c)"), 1.0), writes=[vh4])
                for dh in range(2):
                    P.dma('pool', wv, wv[:], W["da_w_in"], win_v[:, :, 2048 + dh * 512:2048 + (dh + 1) * 512])
                    for t in range(NT):
                        pv = rot("pd", S.bank[4:6])
                        for k in range(8):
                            P.mm(pv, pv[:], hnT, hnT[:, k, t * 128:(t + 1) * 128], wv, wv[:, k, :],
                                 start=(k == 0), stop=(k == 7))
                        P.act(vh4, vh4[:, :, t, 0:128], pv, pv[:].rearrange("p (h f) -> p h f", h=4), AF.Copy)
                    for hh in range(4):
                        hd = dh * 4 + hh
                        P.dma('sp', vin[hd], vin[hd][:, :], vh4, vh4[:, hh].rearrange("p t j -> p (t j)"))
                    for hh in range(4):
                        hd = dh * 4 + hh
                        P.coll("AllGather", vin[hd], vin[hd][:, :], vall[hd], vall[hd][:, :], GROUPS)
            if ATTN_STOP == "X":
                return
            on = [P.sb("on%d" % t, [128, D], BF16) for t in range(NT)]
            with P.scope():
                ks = [[P.sb("ks%d_%d" % (i, m), [128, 2048], BF16) for m in range(2)] for i in range(2)]
                for i in range(2):
                    for m in range(2):
                        P.op('dve', lambda g, b=ks[i][m]: g.memset(b[:], 0.0), writes=[ks[i][m]])
                vs_ = [P.sb("vs%d" % i, [128, 16, 136], BF16) for i in range(2)]
                ETs = [P.sb("ET%d" % i, [128, 512], BF16) for i in range(4)]
                of = P.sb("of", [128, 8, 128], F32)
                o1 = P.sb("o1", [128, 128], F32)
                rl = P.sb("rl", [128, 4], F32)
                accb = S.bank[0:6]
                for h in range(ATTN_HEADS):
                    for qh in range(2):
                        started = [False] * 6
                        nsrc = min(ATTN_SRCS, 4)
                        slots = {}

                        def load_src(src, h=h):
                            kb = ks[src % 2]
                            vb = vs_[src % 2]
                            if src == 0:
                                for m in range(2):
                                    P.dma('sp', kb[m], kb[m][m * 64:(m + 1) * 64, :], kin[h], kin[h][m * 64:(m + 1) * 64, :])
                                P.dma('sp', vb, vb[:, :, 0:129], vin[h], vin[h][:, :].rearrange("p (t j) -> p t j", j=129))
                            else:
                                r0 = (src - 1) * 128
                                for m in range(2):
                                    P.dma('sp', kb[m], kb[m][m * 64:(m + 1) * 64, :], kall[h],
                                          kall[h][r0 + m * 64:r0 + (m + 1) * 64, :])
                                P.dma('sp', vb, vb[:, :, 0:129], vall[h], vall[h][r0:r0 + 128, :].rearrange("p (t j) -> p t j", j=129))
                                P.ts('dve', vb, vb[:, :, 0:129], vb, vb[:, :, 0:129], S.cfg[:, 7 + src:8 + src], None, ALU.mult,
                                     extra_reads=[S.cfg])
                            slots[src] = (kb, vb)

                        def emit_pv(u):
                            (src, kt, first, nv, mp, ET) = u
                            kb, vb = slots[src]
                            for i in range(nv):
                                a = mp * 8 + (first + i - qh * 8)
                                bk = accb[a // 3]
                                c0 = (a % 3) * 160
                                st = not started[a // 3]
                                started[a // 3] = True
                                P.mm(bk, bk[:, c0:c0 + 129], ET, ET[:, i * 128:(i + 1) * 128],
                                     vb, vb[:, kt, 0:129], start=st, stop=False, inc=(i == nv - 1),
                                     skip_group_check=True)

                        load_src(0)
                        if nsrc > 1:
                            load_src(1)
                        pending = None
                        for src in range(nsrc):
                            kb, vb = slots[src]
                            first_unit = True
                            for kt in range(16):
                                for qg in range(2):
                                    s0 = qh * 8 + qg * 4
                                    first = max(s0, kt) if src == 0 else s0
                                    nv = s0 + 4 - first
                                    if nv <= 0:
                                        continue
                                    for mp in range(2):
                                        pS = rot("pS", S.bank[6:8])
                                        prow = slice(mp * 64, (mp + 1) * 64)
                                        diag = (src == 0 and first == kt)
                                        P.mm(pS, pS[:, 0:nv * 128], kb[mp], kb[mp][:, kt * 128:(kt + 1) * 128],
                                             qT, qT[:, h, first * 128:(first + nv) * 128],
                                             start=True, stop=not diag)
                                        if diag:
                                            P.mm(pS, pS[:, 0:128], S.ident, S.ident[:], S.negm, S.negm[:],
                                                 start=False, stop=True)
                                        ET = rot("ET", ETs)
                                        P.act(ET, ET[:, 0:nv * 128], pS, pS[:, 0:nv * 128], AF.Exp)
                                        if pending is not None:
                                            emit_pv(pending)
                                        pending = (src, kt, first, nv, mp, ET)
                                        if first_unit:
                                            first_unit = False
                                            if src >= 1 and src + 1 < nsrc:
                                                load_src(src + 1)
                        emit_pv(pending)
                        ssb = rot("ss", S.ss)
                        for qs in range(8):
                            a1, a2 = qs, 8 + qs
                            b1, c1 = accb[a1 // 3], (a1 % 3) * 160
                            b2, c2 = accb[a2 // 3], (a2 % 3) * 160
                            P.op('dve', lambda g, b1=b1, c1=c1: g.reciprocal(rl[:, 0:1], b1[:, c1 + 128:c1 + 129]),
                                 reads=[b1], writes=[rl])
                            P.op('dve', lambda g, b2=b2, c2=c2: g.reciprocal(rl[:, 1:2], b2[:, c2 + 128:c2 + 129]),
                                 reads=[b2], writes=[rl])
                            P.tt('dve', rl, rl[:, 2:3], rl, rl[:, 1:2], lam, lam[:, 5:6], ALU.mult)
                            P.ts('dve', o1, o1[:], b1, b1[:, c1:c1 + 128], rl[:, 0:1], None, ALU.mult, extra_reads=[rl])
                            P.stt('dve', of, of[:, qs, :], b2, b2[:, c2:c2 + 128], rl[:, 2:3], o1, o1[:],
                                  ALU.mult, ALU.add, extra_reads=[rl])
                            P.act(S.junk, S.junk[:, 0:128], of, of[:, qs, :], AF.Square, extra_writes=[ssb],
                                  accum_out=ssb[:, qs:qs + 1])
                        rstd_cols(ssb, 8, 128, lnexp=True)
                        for qs in range(8):
                            ob = on[qh * 8 + qs]
                            P.stt('dve', ob, ob[:, h * 128:(h + 1) * 128], of, of[:, qs, :], ssb[:, 16 + qs:17 + qs],
                                  gsub, gsub[:], ALU.mult, ALU.mult, extra_reads=[ssb])
            with P.scope():
                onT = P.sb("onT", [128, 8, 1024], BF16)
                wo = P.sb("wo", [128, 8, 1024], BF16)
                P.dma('pool', wo, wo[:], W["da_w_out"], wout_v)
                for tg in range(2):
                    for j in range(8):
                        ob = on[tg * 8 + j]
                        pT = rot("pT", S.bank[6:8])
                        for k in range(8):
                            P.tr(pT, bf(pT)[:, k * 128:(k + 1) * 128], ob, ob[:, k * 128:(k + 1) * 128],
                                 S.ident, S.ident[:], inc=(k == 7))
                        P.act(onT, onT[:, :, j * 128:(j + 1) * 128], pT, bf(pT).rearrange("p (k n) -> p k n", k=8), AF.Copy)
                    proj_out_add(onT, wo, S.h[tg * 8:tg * 8 + 8], 1.0)

    def ssd():
        win_v = W["ssd_w_in"][0].rearrange("(k p) n -> p k n", p=128)
        wout_v = W["ssd_w_out"][0].rearrange("(c p) n -> p c n", p=128)
        XB = 2048
        gc = S.gcol[("mix_norm", 1)]
        with P.scope():
            hnT = P.sb("hnT", [128, 8, 2048], BF16)
            hal = P.sb("hal", [128, 8, 4], BF16)
            dt = P.sb("dt", [128, 16, 32], F32)
            a = P.sb("a", [128, 16, 32], F32)
            acat = P.sb("acat", [128, 16, 64], F32)
            eacs = P.sb("eacs", [128, 16, 32], F32)
            dte = P.sb("dte", [128, 16, 32], F32)
            eat = P.sb("eat", [128, 16, 32], F32)
            dtb = P.sb("dtb", [128, 32], F32)
            Ab = P.sb("Ab", [128, 32], F32)
            Db = P.sb("Db", [128, 32], F32)
            Uf = P.sb("Uf", [128, 128], F32)
            Ub = P.sb("Ub", [128, 128], BF16)
            nones = P.sb("nones", [128, 128], BF16)
            Sin = P.sb("Sin", [128, 2048], F32)
            dsum = P.sb("dsum", [128, 32], F32)
            P.dma('sp', Uf, Uf[:], utri_d, utri_d[:, :])
            P.dma('pool', Ub, Ub[:], utri_d, utri_d[:, :])
            P.op('dve', lambda g: g.memset(nones[:], -1.0), writes=[nones])
            P.op('dve', lambda g: g.memset(Sin[:], 0.0), writes=[Sin])
            P.dma('sp', dtb, dtb[:], W["ssd_dt_bias"], W["ssd_dt_bias"][0, :].partition_broadcast(128))
            P.dma('sp', Ab, Ab[:], W["ssd_A_log"], W["ssd_A_log"][0, :].partition_broadcast(128))
            P.dma('sp', Db, Db[:], W["ssd_D"], W["ssd_D"][0, :].partition_broadcast(128))
            P.act(Ab, Ab[:], Ab, Ab[:], AF.Exp)
            P.ts('dve', Ab, Ab[:], Ab, Ab[:], -1.0, None, ALU.mult)
            with P.scope():
                hs_l = [P.sb("hs%d" % i, [128, D], BF16) for i in range(2)]
                for tg in range(2):
                    ssb, o = rstd_group(S.h[tg * 8:tg * 8 + 8])
                    for j in range(8):
                        hb = S.h[tg * 8 + j]
                        hs = rot("hs", hs_l)
                        pT = rot("pT", S.bank[6:8])
                        P.ts('dve', hs, hs[:], hb, hb[:], ssb[:, o + j:o + j + 1], None, ALU.mult, extra_reads=[ssb])
                        for k in range(8):
                            P.tr(pT, bf(pT)[:, k * 128:(k + 1) * 128], hs, hs[:, k * 128:(k + 1) * 128],
                                 S.ident, S.ident[:], inc=(k == 7))
                        tcol = (tg * 8 + j) * 128
                        P.tt('dve', hnT, hnT[:, :, tcol:tcol + 128], pT, bf(pT).rearrange("p (k n) -> p k n", k=8),
                             gc, gc[:].to_broadcast([128, 8, 128]), ALU.mult)
                hr = P.sb("hr", [128, 4, 32], BF16)
                hacc = P.sb("hacc", [128, 32], F32)
                P.dma('sp', halin, halin[:, :].rearrange("p (k t) -> p k t", k=8), hnT, hnT[:, :, 2044:2048])
                P.coll("AllGather", halin, halin[:, :], halall, halall[:, :], GROUPS)
                P.dma('sp', hr, hr[:], halall, halall[:, :].rearrange("(r p) c -> p r c", p=128))
                P.ts('dve', hacc, hacc[:], hr, hr[:, 0, :], S.cfg[:, 4:5], None, ALU.mult, extra_reads=[S.cfg])
                for r in range(1, 4):
                    P.stt('dve', hacc, hacc[:], hr, hr[:, r, :], S.cfg[:, 4 + r:5 + r], hacc, hacc[:], ALU.mult, ALU.add,
                          extra_reads=[S.cfg])
                P.copy('dve', hal, hal[:].rearrange("p k t -> p (k t)"), hacc, hacc[:])
            if SSD_STOP <= 1:
                return
            with P.scope():
                wdt = P.sb("wdt", [128, 8, 32], BF16)
                ahi = P.sb("ahi", [128, 16, 32], BF16)
                alo = P.sb("alo", [128, 16, 32], BF16)
                tmp = P.sb("tmp", [128, 16, 32], F32)
                P.dma('pool', wdt, wdt[:], W["ssd_w_in"], win_v[:, :, 6144:6176])
                for t in range(NT):
                    pd_ = rot("pd", S.bank[4:6])
                    for k in range(8):
                        P.mm(pd_, pd_[:, 0:32], hnT, hnT[:, k, t * 128:(t + 1) * 128], wdt, wdt[:, k, :],
                             start=(k == 0), stop=(k == 7))
                    P.tt('dve', dt, dt[:, t, :], pd_, pd_[:, 0:32], dtb, dtb[:], ALU.add)
                P.act(dt, dt[:], dt, dt[:], AF.Exp)
                P.act(dt, dt[:], dt, dt[:], AF.Ln, bias=1.0)
                P.tt('dve', a, a[:], dt, dt[:], Ab, Ab[:].unsqueeze(1).to_broadcast([128, 16, 32]), ALU.mult)
                P.copy('dve', ahi, ahi[:], a, a[:])
                P.tt('dve', tmp, tmp[:], a, a[:], ahi, ahi[:], ALU.subtract)
                P.copy('dve', alo, alo[:], tmp, tmp[:])
                for t in range(NT):
                    pd_ = rot("pd", S.bank[4:6])
                    P.mm(pd_, pd_[:, 0:32], Ub, Ub[:], ahi, ahi[:, t, :], start=True, stop=False)
                    P.mm(pd_, pd_[:, 0:32], Ub, Ub[:], alo, alo[:, t, :], start=False, stop=True)
                    P.copy('dve', acat, acat[:, t, 0:32], pd_, pd_[:, 0:32])
                    pd_ = rot("pd", S.bank[4:6])
                    P.mm(pd_, pd_[:, 0:32], S.ones, S.ones[:], ahi, ahi[:, t, :], start=True, stop=False)
                    P.mm(pd_, pd_[:, 0:32], S.ones, S.ones[:], alo, alo[:, t, :], start=False, stop=True)
                    P.copy('dve', acat, acat[:, t, 32:64], pd_, pd_[:, 0:32])
                P.act(eacs, eacs[:], acat, acat[:, :, 0:32], AF.Exp)
                P.act(eat, eat[:], acat, acat[:, :, 32:64], AF.Exp)
                P.tt('dve', tmp, tmp[:], acat, acat[:, :, 32:64], acat, acat[:, :, 0:32], ALU.subtract)
                P.act(dte, dte[:], tmp, tmp[:], AF.Exp)
                P.copy('dve', dsum, dsum[:], acat, acat[:, 0, 32:64])
                for t in range(1, NT):
                    P.tt('dve', dsum, dsum[:], dsum, dsum[:], acat, acat[:, t, 32:64], ALU.add)

            if SSD_STOP <= 2:
                return

            def group_pass(g, final):
                with P.scope():
                    xcbc = P.sb("xcbc", [128, 2, 2048], BF16)
                    xs_tok = P.sb("xstok", [128, 16, 256], BF16)
                    B_tok = P.sb("Btok", [128, 16, 128], BF16)
                    prev = P.sb("prev", [128, 256], F32)
                    prevb = P.sb("prevb", [128, 256], BF16)
                    dtd = P.sb("dtd", [128, 16, 4], F32)
                    hs4 = slice(g * 4, g * 4 + 4)
                    cbase = (XB + g * 256, XB + g * 256 + 128, XB + 2048 + g * 128, XB + 3072 + g * 128)
                    with P.scope():
                        wx = P.sb("wx", [128, 8, 512], BF16)
                        cw = P.sb("cw", [128, 4, 4], F32)
                        cb = P.sb("cb", [128, 4], F32)
                        u_l = [P.sb("u%d" % i, [128, 2052], F32) for i in range(2)]
                        acc_l = [P.sb("acc%d" % i, [128, 2048], F32) for i in range(2)]
                        chunks = (0, 1, 2, 3) if final else (0, 1, 2)
                        xcx = P.sb("xcx", [128, 2, 2048], BF16)
                        for ci in chunks:
                            P.dma('pool', wx, wx[:, :, ci * 128:(ci + 1) * 128], W["ssd_w_in"],
                                  win_v[:, :, cbase[ci]:cbase[ci] + 128], accumulate_w=True)
                            cch = cbase[ci] - XB
                            P.dma('sp', cw, cw[:, ci, :], W["ssd_conv_w"],
                                  W["ssd_conv_w"][0][:, cch:cch + 128].rearrange("k p -> p k"), accumulate_w=True)
                            P.dma('sp', cb, cb[:, ci:ci + 1], W["ssd_conv_b"],
                                  W["ssd_conv_b"][0, cch:cch + 128].rearrange("(p o) -> p o", o=1), accumulate_w=True)
                        for ci in chunks:
                            u, acc = u_l[ci % 2], acc_l[ci % 2]
                            ph = rot("pd", S.bank[4:6])
                            for k in range(8):
                                P.mm(ph, ph[:, 0:4], wx, wx[:, k, ci * 128:(ci + 1) * 128], hal, hal[:, k, :],
                                     start=(k == 0), stop=(k == 7))
                            P.act(u, u[:, 0:3], ph, ph[:, 1:4], AF.Copy)
                            for ts_ in range(4):
                                pu_ = rot("pg", S.bank[0:4])
                                for k in range(8):
                                    P.mm(pu_, pu_[:], wx, wx[:, k, ci * 128:(ci + 1) * 128],
                                         hnT, hnT[:, k, ts_ * 512:(ts_ + 1) * 512], start=(k == 0), stop=(k == 7))
                                P.act(u, u[:, 3 + ts_ * 512:3 + (ts_ + 1) * 512], pu_, pu_[:], AF.Copy)
                            P.ts('dve', acc, acc[:], u, u[:, 0:2048], cw[:, ci, 0:1], cb[:, ci:ci + 1], ALU.mult, ALU.add,
                                 extra_reads=[cw, cb])
                            for k in range(1, 4):
                                P.stt('dve', acc, acc[:], u, u[:, k:k + 2048], cw[:, ci, k:k + 1], acc, acc[:],
                                      ALU.mult, ALU.add, extra_reads=[cw])
                            if ci < 2:
                                P.act(xcx, xcx[:, ci, :], acc, acc[:], AF.Silu)
                            else:
                                P.act(xcbc, xcbc[:, ci - 2, :], acc, acc[:], AF.Silu)
                        for t in range(NT if SSD_STOP > 2.2 else 0):
                            pT = rot("pT", S.bank[6:8])
                            tsl_ = slice(t * 128, (t + 1) * 128)
                            for ci in range(3):
                                srcb = xcx if ci < 2 else xcbc
                                P.tr(pT, bf(pT)[:, ci * 128:(ci + 1) * 128], srcb, srcb[:, ci % 2 if ci < 2 else 0, tsl_],
                                     S.ident, S.ident[:], inc=(ci == 2))
                            P.copy('dve', xs_tok, xs_tok[:, t, :], pT, bf(pT)[:, 0:256])
                            P.act(B_tok, B_tok[:, t, :], pT, bf(pT)[:, 256:384], AF.Copy)
                    if SSD_STOP <= 2.5:
                        return
                    xt = P.sb("xt", [128, 16, 256], BF16)
                    xtd = P.sb("xtd", [128, 16, 256], BF16)
                    v4 = lambda ap: ap.rearrange("p t (h f) -> p t h f", h=4)
                    bc4 = lambda ap: ap.unsqueeze(3).to_broadcast([128, 16, 4, 64])
                    P.tt('dve', dtd, dtd[:], dt, dt[:, :, hs4], dte, dte[:, :, hs4], ALU.mult)
                    if final:
                        P.tt('dve', xt, v4(xt[:]), xs_tok, v4(xs_tok[:]), dt, bc4(dt[:, :, hs4]), ALU.mult)
                    P.tt('dve', xtd, v4(xtd[:]), xs_tok, v4(xs_tok[:]), dtd, bc4(dtd[:]), ALU.mult)
                    if SSD_STOP <= 2.7:
                        return
                    if final:
                        P.copy('dve', prev, prev[:], Sin, Sin[:, g * 256:(g + 1) * 256])
                    else:
                        P.op('dve', lambda e: e.memset(prev[:], 0.0), writes=[prev])
                    if final:
                        wz = P.sb("wz", [128, 8, 256], BF16)
                        wo = P.sb("wo", [128, 2, D], BF16)
                        AU_l = [P.sb("AU%d" % i, [128, 4, 128], F32) for i in range(2)]
                        AUh_l = [P.sb("AUh%d" % i, [128, 4, 128], BF16) for i in range(2)]
                        AUl_l = [P.sb("AUl%d" % i, [128, 4, 128], BF16) for i in range(2)]
                        dec_l = [P.sb("dec%d" % i, [128, 4, 128], F32) for i in range(2)]
                        MT_l = [P.sb("MT%d" % i, [128, 4, 128], BF16) for i in range(2)]
                        yb_l = [P.sb("yb%d" % i, [128, 256], F32) for i in range(2)]
                        zs_l = [P.sb("zs%d" % i, [128, 256], F32) for i in range(2)]
                        ynb_l = [P.sb("ynb%d" % i, [128, 256], BF16) for i in range(2)]
                        ynT_l = [P.sb("ynT%d" % i, [128, 2, 128], BF16) for i in range(2)]
                        prevb_l = [P.sb("prevb%d" % i, [128, 256], BF16) for i in range(2)]
                        gng = P.sb("gng", [128, 256], F32)
                        P.dma('sp', gng, gng[:], W["ssd_gnorm"], W["ssd_gnorm"][0, g * 256:(g + 1) * 256].partition_broadcast(128))
                        P.dma('pool', wz, wz[:], W["ssd_w_in"], win_v[:, :, g * 256:(g + 1) * 256])
                        P.dma('pool', wo, wo[:], W["ssd_w_out"], wout_v[:, 2 * g:2 * g + 2, :])
                    y4 = lambda ap: ap.rearrange("p (h f) -> p h f", h=4)
                    b4 = lambda ap: ap.unsqueeze(2).to_broadcast([128, 4, 64])

                    def state_update(c):
                        pS_ = S.bank[5]
                        P.mm(pS_, pS_[:, 0:256], B_tok, B_tok[:, c, :], xtd, xtd[:, c, :], start=True, stop=True)
                        P.tt('dve', prev, y4(prev[:]), prev, y4(prev[:]), eat, b4(eat[:, c, hs4]), ALU.mult)
                        P.tt('dve', prev, prev[:], prev, prev[:], pS_, pS_[:, 0:256], ALU.add)

                    def emit_A(c):
                        csl = slice(c * 128, (c + 1) * 128)
                        i2 = c % 2
                        AU, AUh, AUl, dec, MT = AU_l[i2], AUh_l[i2], AUl_l[i2], dec_l[i2], MT_l[i2]
                        zs, prevb, yA = zs_l[i2], prevb_l[i2], yA_l[i2]
                        P.copy('dve', prevb, prevb[:], prev, prev[:])
                        pCB = S.bank[0]
                        P.mm(pCB, pCB[:, 0:128], xcbc, xcbc[:, 0, csl], xcbc, xcbc[:, 1, csl], start=True, stop=True)
                        P.tt('pool', AU, AU[:], a, a[:, c, hs4].unsqueeze(2).to_broadcast([128, 4, 128]),
                             Uf, Uf[:].unsqueeze(1).to_broadcast([128, 4, 128]), ALU.mult)
                        P.copy('pool', AUh, AUh[:], AU, AU[:])
                        P.tt('pool', AU, AU[:], AU, AU[:], AUh, AUh[:], ALU.subtract)
                        P.copy('pool', AUl, AUl[:], AU, AU[:])
                        pR = S.bank[1]
                        P.mm(pR, pR[:], S.ones, S.ones[:], AUh, AUh[:].rearrange("p h l -> p (h l)"),
                             start=True, stop=False, inc=False)
                        P.mm(pR, pR[:], S.ones, S.ones[:], AUl, AUl[:].rearrange("p h l -> p (h l)"),
                             start=False, stop=False, inc=False)
                        for hh in range(4):
                            cs_ = slice(hh * 128, (hh + 1) * 128)
                            P.mm(pR, pR[:, cs_], AUh, AUh[:, hh, :], nones, nones[:], start=False, stop=False, inc=False)
                            P.mm(pR, pR[:, cs_], AUl, AUl[:, hh, :], nones, nones[:], start=False, stop=False, inc=False)
                            P.mm(pR, pR[:, cs_], S.ident, S.ident[:], S.negm, S.negm[:], start=False, stop=(hh == 3),
                                 inc=(hh == 3))
                        P.act(dec, dec[:].rearrange("p h l -> p (h l)"), pR, pR[:], AF.Exp)
                        P.tt('dve', MT, MT[:], dec, dec[:], pCB, pCB[:, 0:128].unsqueeze(1).to_broadcast([128, 4, 128]),
                             ALU.mult)
                        pY = S.bank[2]
                        for hh in range(4):
                            P.mm(pY, pY[:, hh * 64:(hh + 1) * 64], MT, MT[:, hh, :], xt, xt[:, c, hh * 64:(hh + 1) * 64],
                                 start=(hh == 0), stop=(hh == 3), inc=(hh == 3), skip_group_check=True)
                        P.tt('dve', dx, y4(dx[:]), xs_tok, y4(xs_tok[:, c, :]), Db, b4(Db[:, hs4]), ALU.mult)
                        P.tt('dve', yA, yA[:], dx, dx[:], pY, pY[:, 0:256], ALU.add)
                        pZ = S.bank[4]
                        for k in range(8):
                            P.mm(pZ, pZ[:, 0:256], hnT, hnT[:, k, csl], wz, wz[:, k, :], start=(k == 0), stop=(k == 7))
                        P.act(zs, zs[:], pZ, pZ[:, 0:256], AF.Exp, scale=-1.0)
                        P.act(zs, zs[:], zs, zs[:], AF.Ln, bias=1.0)
                        P.act(zs, zs[:], zs, zs[:], AF.Exp, scale=-1.0)
                        P.tt('dve', zs, zs[:], zs, zs[:], pZ, pZ[:, 0:256], ALU.mult)
                        state_update(c)

                    def emit_B(c):
                        csl = slice(c * 128, (c + 1) * 128)
                        i2 = c % 2
                        zs, prevb, yA = zs_l[i2], prevb_l[i2], yA_l[i2]
                        yb, ynb, ynT = yb_l[i2], ynb_l[i2], ynT_l[i2]
                        pO = S.bank[3]
                        P.mm(pO, pO[:, 0:256], xcbc, xcbc[:, 1, csl], prevb, prevb[:], start=True, stop=True)
                        P.tt('dve', yb, y4(yb[:]), pO, y4(pO[:, 0:256]), eacs, b4(eacs[:, c, hs4]), ALU.mult)
                        P.tt('dve', yb, yb[:], yb, yb[:], yA, yA[:], ALU.add)
                        P.tt('dve', yb, yb[:], yb, yb[:], zs, zs[:], ALU.mult)
                        ssb = rot("ss", S.ss)
                        P.act(S.junk, S.junk[:, 0:256], yb, yb[:], AF.Square, extra_writes=[ssb], accum_out=ssb[:, 0:1])
                        rstd_cols(ssb, 1, 256, lnexp=True)
                        P.stt('dve', ynb, ynb[:], yb, yb[:], ssb[:, 16:17], gng, gng[:],
                              ALU.mult, ALU.mult, extra_reads=[ssb])
                        pT = S.bank[7]
                        for cc in range(2):
                            P.tr(pT, bf(pT)[:, cc * 128:(cc + 1) * 128], ynb, ynb[:, cc * 128:(cc + 1) * 128],
                                 S.ident, S.ident[:], inc=(cc == 1))
                        P.copy('dve', ynT, ynT[:].rearrange("p c n -> p (c n)"), pT, bf(pT)[:, 0:256])
                        hb = S.h[c]
                        for dh in range(2):
                            pd_ = S.bank[6]
                            for cc in range(2):
                                P.mm(pd_, pd_[:], ynT, ynT[:, cc, :], wo, wo[:, cc, dh * 512:(dh + 1) * 512],
                                     start=(cc == 0), stop=(cc == 1))
                            P.tt('dve', hb, hb[:, dh * 512:(dh + 1) * 512], hb, hb[:, dh * 512:(dh + 1) * 512],
                                 pd_, pd_[:], ALU.add)

                    if final:
                        yA_l = [P.sb("yA%d" % i, [128, 256], F32) for i in range(2)]
                        dx = P.sb("dx", [128, 256], F32)
                        for c in range(NT):
                            emit_A(c)
                            if c >= 1:
                                emit_B(c - 1)
                        emit_B(NT - 1)
                    else:
                        for c in range(NT):
                            state_update(c)
                    if not final:
                        P.dma('sp', ssdin, ssdin[:, g * 256:(g + 1) * 256], prev, prev[:], accumulate_w=True)

            for g in range(8 if SSD_STOP > 3 else 1):
                group_pass(g, False)
            if SSD_STOP < 3:
                return
            if SSD_STOP <= 3:
                return
            P.dma('sp', ssdd, ssdd[:, :], dsum, dsum[:])
            P.coll("AllGather", ssdin, ssdin[:, :], ssdall, ssdall[:, :], GROUPS)
            P.coll("AllGather", ssdd, ssdd[:, :], ssddall, ssddall[:, :], GROUPS)
            with P.scope():
                fr = [P.sb("fr%d" % i, [128, 2080], F32) for i in range(2)]
                dm = P.sb("dm", [128, 32], F32)
                for r in range(4):
                    f_ = rot("fr", fr)
                    P.dma('sp', f_, f_[:, 0:2048], ssdall, ssdall[r * 128:(r + 1) * 128, :])
                    P.dma('sp', f_, f_[:, 2048:2080], ssddall, ssddall[r * 128:(r + 1) * 128, :], accumulate_w=True)
                    P.act(dm, dm[:], f_, f_[:, 2048:2080], AF.Exp)
                    P.ts('dve', dm, dm[:], dm, dm[:], S.cfg[:, 8 + r:9 + r], S.cfg[:, 12 + r:13 + r], ALU.mult, ALU.add,
                         extra_reads=[S.cfg])
                    s4 = lambda ap: ap.rearrange("p (h f) -> p h f", h=32)
                    P.tt('dve', Sin, s4(Sin[:]), Sin, s4(Sin[:]), dm, dm[:].unsqueeze(2).to_broadcast([128, 32, 64]), ALU.mult)
                    P.stt('dve', Sin, Sin[:], f_, f_[:, 0:2048], S.cfg[:, 8 + r:9 + r], Sin, Sin[:], ALU.mult, ALU.add,
                          extra_reads=[S.cfg])
            if SSD_STOP <= 4:
                return
            for g in range(8 if SSD_STOP > 5 else 1):
                group_pass(g, True)

    def final_out():
        ssb, o = rstd_group(S.h)
        for t in range(NT):
            hb = S.h[t]
            P.stt('dve', hb, hb[:], hb, hb[:], ssb[:, o + t:o + t + 1], S.gfin, S.gfin[:], ALU.mult, ALU.mult,
                  extra_reads=[ssb])
            P.dma('sp', out, out[t * 128:(t + 1) * 128, :], hb, hb[:], accumulate_w=True)
        P.wait_all('sp', [out])

    for st in stages:
        if st.startswith("ffn"):
            ffn(int(st[-1]), st[:4])
        elif st == "attn":
            attn()
        elif st.startswith("xa"):
            xattn(int(st[-1]))
        elif st == "ssd":
            ssd()
    final_out()
    P.finish()
    es.close()
    nc.used_weights = list(W.keys())
    return nc


def make_in_maps(inputs, used=None):
    x = np.asarray(inputs["x"], dtype=np.float32)
    memv = np.asarray(inputs["mem"], dtype=np.float32)
    posv = np.asarray(inputs["positions"], dtype=np.int32)
    ident = np.eye(128, dtype=np.float32)
    kk, qq = np.meshgrid(np.arange(128), np.arange(128), indexing="ij")
    negmask = np.where(kk > qq, NEG, 0.0).astype(np.float32)
    utri = np.where(kk <= qq, 1.0, 0.0).astype(np.float32)
    inv = (1.0 / (10000.0 ** (np.arange(0, 64, 2, dtype=np.float32) / 64))).astype(np.float32)
    ropec = np.zeros((128, 4), np.float32)
    for p in range(128):
        ropec[p, 0] = inv[p % 32]
        ropec[p, 1] = -1.0 if (p % 64) < 32 else 1.0
    shared = {k: np.ascontiguousarray(np.asarray(inputs[k], dtype=np.float32)) for k, _ in WNAMES
              if used is None or k in used}
    maps = []
    for c in range(NCORES):
        b, q = c // 4, c % 4
        m = dict(shared)
        m["x"] = np.ascontiguousarray(x[b, q * TOK:(q + 1) * TOK, :])
        m["mem"] = np.ascontiguousarray(memv[b])
        m["pos"] = np.ascontiguousarray(posv[b:b + 1, q * TOK:(q + 1) * TOK])
        cfgv = np.zeros((128, 16), np.float32)
        for r in range(4):
            cfgv[:, r] = 0.0 if r < q else NEG
            cfgv[:, 4 + r] = 1.0 if r == q - 1 else 0.0
            cfgv[:, 8 + r] = 1.0 if r < q else 0.0
            cfgv[:, 12 + r] = 0.0 if r < q else 1.0
        m["cfg"] = cfgv
        m["ropec"] = ropec
        m["ident"] = ident
        m["negmask"] = negmask
        m["utri"] = utri
        maps.append(m)
    return maps


def kernel(**inputs):
    nc = build()
    res = run_bass_kernel_spmd(nc, make_in_maps(inputs, nc.used_weights), core_ids=list(range(NCORES)))
    outp = np.empty((2, 8192, D), dtype=np.float32)
    for c in range(NCORES):
        b, q = c // 4, c % 4
        outp[b, q * TOK:(q + 1) * TOK, :] = res.results[c]["out"]
    return outp
```
